# Optimizing a Trainium2 kernel written in Bass

```python
import jax, jax.numpy as jnp
from jax import lax
import numpy as np

D_MODEL = 2048
BATCH = 4
SEQ = 8192
DEPTH = 1

MIX_WIDTH = D_MODEL
ATTN_WIDTH = MIX_WIDTH // 2
REC_WIDTH = MIX_WIDTH - ATTN_WIDTH
N_ATTN_HEADS = 8
V_HEAD_DIM = ATTN_WIDTH // N_ATTN_HEADS
QK_NOPE_DIM = 128
QK_ROPE_DIM = 64
QK_HEAD_DIM = QK_NOPE_DIM + QK_ROPE_DIM
Q_LORA_RANK = 512
KV_LORA_RANK = 256
ROPE_THETA = 10000.0
Q_BLOCK = 128
N_REC_HEADS = 8
REC_HEAD_DIM = REC_WIDTH // N_REC_HEADS
CONV_WIDTH = 4
LRU_C = 8.0
N_PEER_HEADS = 8
PEER_TOPK = 16
N_KEYS = 128
N_EXPERTS = N_KEYS * N_KEYS
PEER_QUERY_DIM = 256
PEER_HALF = PEER_QUERY_DIM // 2
TOKEN_CHUNK = 128
EPS = 1e-6
OFF_CQ = 0
OFF_CKV = OFF_CQ + Q_LORA_RANK
OFF_KR = OFF_CKV + KV_LORA_RANK
OFF_XR = OFF_KR + QK_ROPE_DIM
OFF_YG = OFF_XR + REC_WIDTH
IN_COLS = OFF_YG + REC_WIDTH

kernel_name = "hymba_mla_rglru_peer_layer"


def rms_norm(t, g):
    tf = t.astype(jnp.float32)
    y = tf * lax.rsqrt(jnp.mean(tf * tf, axis=-1, keepdims=True) + EPS)
    return (y * g.astype(jnp.float32)).astype(t.dtype)


def rope_tables(positions):
    half = QK_ROPE_DIM // 2
    inv_freq = ROPE_THETA ** (-jnp.arange(half, dtype=jnp.float32) / half)
    ang = positions.astype(jnp.float32)[..., None] * inv_freq
    return jnp.cos(ang), jnp.sin(ang)


def apply_rope(t, cos, sin):
    tf = t.astype(jnp.float32)
    t1, t2 = jnp.split(tf, 2, axis=-1)
    out = jnp.concatenate([t1 * cos - t2 * sin, t1 * sin + t2 * cos], axis=-1)
    return out.astype(t.dtype)


def causal_block_attention(q, k, v):
    S = q.shape[2]
    scale = QK_HEAD_DIM ** -0.5
    diag = jnp.tril(jnp.ones((Q_BLOCK, Q_BLOCK), dtype=bool))
    outs = []
    for blk in range(S // Q_BLOCK):
        start = blk * Q_BLOCK
        end = start + Q_BLOCK
        s = jnp.einsum('bhqd,bhkd->bhqk', q[:, :, start:end], k[:, :, :end],
                       preferred_element_type=jnp.float32) * scale
        mask = jnp.concatenate([jnp.ones((Q_BLOCK, start), dtype=bool), diag], axis=1)
        p = jax.nn.softmax(jnp.where(mask, s, -jnp.inf), axis=-1).astype(v.dtype)
        outs.append(jnp.einsum('bhqk,bhkd->bhqd', p, v[:, :, :end]))
    return jnp.concatenate(outs, axis=2)


def mla_group(c_q, c_kv, k_rope, cos, sin, q_norm_g, w_uq, kv_norm_g, w_ukv, q_head_g, k_head_g):
    B, S, _ = c_q.shape
    q = (rms_norm(c_q, q_norm_g) @ w_uq).reshape(B, S, N_ATTN_HEADS, QK_HEAD_DIM)
    kv = (rms_norm(c_kv, kv_norm_g) @ w_ukv).reshape(B, S, N_ATTN_HEADS, QK_NOPE_DIM + V_HEAD_DIM)
    k_nope, v = kv[..., :QK_NOPE_DIM], kv[..., QK_NOPE_DIM:]
    k_r = jnp.broadcast_to(k_rope[:, :, None, :], (B, S, N_ATTN_HEADS, QK_ROPE_DIM))
    k = jnp.concatenate([k_nope, k_r], axis=-1)
    q = rms_norm(q, q_head_g)
    k = rms_norm(k, k_head_g)
    c, s = cos[:, :, None, :], sin[:, :, None, :]
    q = jnp.concatenate([q[..., :QK_NOPE_DIM], apply_rope(q[..., QK_NOPE_DIM:], c, s)], axis=-1)
    k = jnp.concatenate([k[..., :QK_NOPE_DIM], apply_rope(k[..., QK_NOPE_DIM:], c, s)], axis=-1)
    o = causal_block_attention(q.transpose(0, 2, 1, 3), k.transpose(0, 2, 1, 3), v.transpose(0, 2, 1, 3))
    return o.transpose(0, 2, 1, 3).reshape(B, S, ATTN_WIDTH)


def _lin_combine(e1, e2):
    a1, b1 = e1
    a2, b2 = e2
    return a1 * a2, a2 * b1 + b2


def rglru_group(x_rec, y_gate, conv_w, conv_b, w_rg, b_rg, w_ig, b_ig, lam):
    B, S, W = x_rec.shape
    xp = jnp.pad(x_rec, ((0, 0), (CONV_WIDTH - 1, 0), (0, 0)))
    xc = conv_b + sum(xp[:, j:j + S] * conv_w[j] for j in range(CONV_WIDTH))
    xh = xc.reshape(B, S, N_REC_HEADS, REC_HEAD_DIM)
    r = jax.nn.sigmoid((jnp.einsum('bshi,hij->bshj', xh, w_rg).reshape(B, S, W) + b_rg).astype(jnp.float32))
    i = jax.nn.sigmoid((jnp.einsum('bshi,hij->bshj', xh, w_ig).reshape(B, S, W) + b_ig).astype(jnp.float32))
    log_a = -LRU_C * r * jax.nn.softplus(-lam.astype(jnp.float32))
    a = jnp.exp(log_a)
    mult = jnp.sqrt(-jnp.expm1(2.0 * log_a))
    b = mult * i * xc.astype(jnp.float32)
    _, h = lax.associative_scan(_lin_combine, (a, b), axis=1)
    out = jax.nn.gelu(y_gate.astype(jnp.float32), approximate=False) * h
    return out.astype(x_rec.dtype)


def peer_ffn(xn, w_q, keys1, keys2, u_tab, v_tab):
    B, S, D = xn.shape
    xt = xn.reshape((B * S) // TOKEN_CHUNK, TOKEN_CHUNK, D)

    def chunk(xc):
        q = (xc @ w_q).reshape(TOKEN_CHUNK, N_PEER_HEADS, 2, PEER_HALF).astype(jnp.float32)
        s1 = jnp.einsum('chd,hnd->chn', q[:, :, 0], keys1.astype(jnp.float32))
        s2 = jnp.einsum('chd,hnd->chn', q[:, :, 1], keys2.astype(jnp.float32))
        v1, i1 = lax.top_k(s1, PEER_TOPK)
        v2, i2 = lax.top_k(s2, PEER_TOPK)
        cand = (v1[..., :, None] + v2[..., None, :]).reshape(TOKEN_CHUNK, N_PEER_HEADS, PEER_TOPK * PEER_TOPK)
        top, pos = lax.top_k(cand, PEER_TOPK)
        e = (jnp.take_along_axis(i1, pos // PEER_TOPK, axis=-1) * N_KEYS
             + jnp.take_along_axis(i2, pos % PEER_TOPK, axis=-1))
        g = jax.nn.softmax(top, axis=-1)
        ug = jnp.take(u_tab, e, axis=0)
        act = jax.nn.gelu(jnp.einsum('chkd,cd->chk', ug, xc).astype(jnp.float32), approximate=False)
        vg = jnp.take(v_tab, e, axis=0)
        return jnp.einsum('chk,chkd->cd', (g * act).astype(xc.dtype), vg)

    return lax.map(chunk, xt).reshape(B, S, D)


def setup_inputs(seed: int = 0) -> dict:
    key = jax.random.key(seed)
    ks = jax.random.split(key, 32)
    L, D = DEPTH, D_MODEL
    f32 = jnp.float32

    def nrm(k, shape, scale):
        return jax.random.normal(k, shape, f32) * scale

    def gain(k, n):
        return 1.0 + 0.05 * jax.random.normal(k, (L, n), f32)

    x = jax.random.normal(ks[0], (BATCH, SEQ, D), f32)
    start = jax.random.randint(ks[1], (BATCH, 1), 0, 4096, dtype=jnp.int32)
    positions = start + jnp.arange(SEQ, dtype=jnp.int32)[None, :]
    ac = jax.random.uniform(ks[2], (L, REC_WIDTH), f32, 0.9, 0.999)
    a0 = ac ** (1.0 / LRU_C)
    lru_lambda = jnp.log(a0) - jnp.log1p(-a0)
    return {
        "x": x,
        "positions": positions,
        "mix_norm_g": gain(ks[3], D),
        "w_in": nrm(ks[4], (L, D, IN_COLS), D ** -0.5),
        "q_norm_g": gain(ks[5], Q_LORA_RANK),
        "w_uq": nrm(ks[6], (L, Q_LORA_RANK, N_ATTN_HEADS * QK_HEAD_DIM), Q_LORA_RANK ** -0.5),
        "kv_norm_g": gain(ks[7], KV_LORA_RANK),
        "w_ukv": nrm(ks[8], (L, KV_LORA_RANK, N_ATTN_HEADS * (QK_NOPE_DIM + V_HEAD_DIM)), KV_LORA_RANK ** -0.5),
        "q_head_norm_g": gain(ks[9], QK_HEAD_DIM),
        "k_head_norm_g": gain(ks[10], QK_HEAD_DIM),
        "conv_w": nrm(ks[11], (L, CONV_WIDTH, REC_WIDTH), CONV_WIDTH ** -0.5),
        "conv_b": nrm(ks[12], (L, REC_WIDTH), 0.01),
        "w_rgate": nrm(ks[13], (L, N_REC_HEADS, REC_HEAD_DIM, REC_HEAD_DIM), REC_HEAD_DIM ** -0.5),
        "b_rgate": nrm(ks[14], (L, REC_WIDTH), 0.1),
        "w_igate": nrm(ks[15], (L, N_REC_HEADS, REC_HEAD_DIM, REC_HEAD_DIM), REC_HEAD_DIM ** -0.5),
        "b_igate": nrm(ks[16], (L, REC_WIDTH), 0.1),
        "lru_lambda": lru_lambda,
        "attn_out_norm_g": gain(ks[17], ATTN_WIDTH),
        "rec_out_norm_g": gain(ks[18], REC_WIDTH),
        "w_out": nrm(ks[19], (L, MIX_WIDTH, D), MIX_WIDTH ** -0.5),
        "ffn_norm_g": gain(ks[20], D),
        "peer_w_q": nrm(ks[21], (L, D, N_PEER_HEADS * PEER_QUERY_DIM), D ** -0.5),
        "peer_keys_1": nrm(ks[22], (L, N_PEER_HEADS, N_KEYS, PEER_HALF), PEER_HALF ** -0.5),
        "peer_keys_2": nrm(ks[23], (L, N_PEER_HEADS, N_KEYS, PEER_HALF), PEER_HALF ** -0.5),
        "peer_u": nrm(ks[24], (L, N_EXPERTS, D), D ** -0.5),
        "peer_v": nrm(ks[25], (L, N_EXPERTS, D), (N_PEER_HEADS * PEER_TOPK) ** -0.5),
    }


def reference(x, positions, mix_norm_g, w_in, q_norm_g, w_uq, kv_norm_g, w_ukv, q_head_norm_g,
              k_head_norm_g, conv_w, conv_b, w_rgate, b_rgate, w_igate, b_igate, lru_lambda,
              attn_out_norm_g, rec_out_norm_g, w_out, ffn_norm_g, peer_w_q, peer_keys_1,
              peer_keys_2, peer_u, peer_v):
    cos, sin = rope_tables(positions)
    for l in range(DEPTH):
        h = rms_norm(x, mix_norm_g[l])
        proj = h @ w_in[l]
        c_q = proj[..., OFF_CQ:OFF_CKV]
        c_kv = proj[..., OFF_CKV:OFF_KR]
        k_rope = proj[..., OFF_KR:OFF_XR]
        x_rec = proj[..., OFF_XR:OFF_YG]
        y_gate = proj[..., OFF_YG:IN_COLS]
        attn = mla_group(c_q, c_kv, k_rope, cos, sin, q_norm_g[l], w_uq[l], kv_norm_g[l], w_ukv[l],
                         q_head_norm_g[l], k_head_norm_g[l])
        rec = rglru_group(x_rec, y_gate, conv_w[l], conv_b[l], w_rgate[l], b_rgate[l], w_igate[l],
                          b_igate[l], lru_lambda[l])
        mixed = jnp.concatenate([rms_norm(attn, attn_out_norm_g[l]), rms_norm(rec, rec_out_norm_g[l])], axis=-1)
        x = x + mixed @ w_out[l]
        x = x + peer_ffn(rms_norm(x, ffn_norm_g[l]), peer_w_q[l], peer_keys_1[l], peer_keys_2[l],
                         peer_u[l], peer_v[l])
    return x
```

```python
import math
from contextlib import ExitStack
import numpy as np
import concourse.bass as bass
import concourse.mybir as mybir
from concourse.bass_utils import run_bass_kernel_spmd

F32 = mybir.dt.float32
BF16 = mybir.dt.bfloat16
U32 = mybir.dt.uint32
I32 = mybir.dt.int32
U8 = mybir.dt.uint8
AF = mybir.ActivationFunctionType
ALU = mybir.AluOpType
AX = mybir.AxisListType
DSZ = {F32: 4, BF16: 2, U32: 4, I32: 4, U8: 1}

D = 2048
EPS = 1e-6
NEG = -30000.0
EPOCH = 30000
N_DMA_SEMS = 12


class Res:
    __slots__ = ("name", "last_w", "readers", "excl")

    def __init__(self, name="", excl=False):
        self.name = name
        self.last_w = None
        self.readers = []
        self.excl = excl


class Op:
    __slots__ = ("eng", "fn", "deps", "dma", "needs_inc", "sem_i", "val")

    def __init__(self, eng, fn, dma):
        self.eng = eng
        self.fn = fn
        self.dma = dma
        self.deps = []
        self.needs_inc = False
        self.sem_i = None
        self.val = None


class Prog:
    ENGS = ("pe", "act", "dve", "pool", "sp")

    def __init__(self, nc):
        self.nc = nc
        self.ops = []
        self.last_dma_on_sem = {}
        self.dma_rr = {e: 0 for e in self.ENGS}
        self.last_op = {e: None for e in self.ENGS}
        self.pending_barrier = {e: None for e in self.ENGS}

    def barrier(self):
        deps = [o for o in self.last_op.values() if o is not None]
        deps += list(self.last_dma_on_sem.values())
        for e in self.ENGS:
            self.pending_barrier[e] = deps

    def add(self, eng, fn, reads=(), writes=(), dma=False):
        op = Op(eng, fn, dma)
        deps = set()
        ex = [r for r in reads if r.excl]
        if ex:
            reads = [r for r in reads if not r.excl]
            writes = list(writes) + ex
        for r in reads:
            if r.last_w is not None:
                deps.add(r.last_w)
        for r in writes:
            if r.last_w is not None:
                deps.add(r.last_w)
            for o in r.readers:
                deps.add(o)
        for r in reads:
            r.readers.append(op)
        for r in writes:
            r.last_w = op
            r.readers = []
        deps.discard(op)
        pb = self.pending_barrier[eng]
        if pb is not None:
            deps.update(pb)
            self.pending_barrier[eng] = None
        if dma:
            k = (eng, self.dma_rr[eng] % N_DMA_SEMS)
            self.dma_rr[eng] += 1
            prev = self.last_dma_on_sem.get(k)
            if prev is not None:
                deps.add(prev)
            self.last_dma_on_sem[k] = op
            op.sem_i = k
        for d in deps:
            if d.eng == "pe" and eng == "pe" and not d.dma and not dma and pb is None:
                continue
            op.deps.append(d)
            d.needs_inc = True
        self.ops.append(op)
        self.last_op[eng] = op
        return op

    def emit(self, stack, final_wait_ops=()):
        nc = self.nc
        cnt = {e: 0 for e in self.ENGS}
        dcnt = {}
        n_ep = {e: 1 for e in self.ENGS}
        for (_, op) in final_wait_ops:
            op.needs_inc = True
        for op in self.ops:
            if op.dma:
                dcnt[op.sem_i] = dcnt.get(op.sem_i, 0) + 16
                op.val = dcnt[op.sem_i]
            elif op.needs_inc:
                c = cnt[op.eng]
                cnt[op.eng] = c + 1
                op.sem_i = (op.eng, "c", c // EPOCH)
                op.val = c % EPOCH + 1
                n_ep[op.eng] = c // EPOCH + 1
        sems = {}
        for e in self.ENGS:
            for k in range(n_ep[e]):
                sems[(e, "c", k)] = stack.enter_context(nc.semaphore(f"c_{e}_{k}"))
        for k in dcnt:
            sems[k] = stack.enter_context(nc.semaphore(f"d_{k[0]}_{k[1]}"))
        per_eng = {e: [] for e in self.ENGS}
        for op in self.ops:
            per_eng[op.eng].append(op)
        final = {e: [] for e in self.ENGS}
        for (e, op) in final_wait_ops:
            final[e].append(op)

        def run(eng_name, eng):
            known = {}
            for op in per_eng[eng_name]:
                need = {}
                for d in op.deps:
                    if d.val > need.get(d.sem_i, 0):
                        need[d.sem_i] = d.val
                for k, v in need.items():
                    if known.get(k, 0) >= v:
                        continue
                    eng.wait_ge(sems[k], v)
                    known[k] = v
                inst = op.fn(eng)
                if op.dma:
                    inst.then_inc(sems[op.sem_i], 16)
                elif op.needs_inc:
                    inst.then_inc(sems[op.sem_i], 1)
            for op in final[eng_name]:
                eng.wait_ge(sems[op.sem_i], op.val)

        block = stack.enter_context(nc.Block())

        @block.tensor
        def _(e):
            run("pe", e)

        @block.scalar
        def _(e):
            run("act", e)

        @block.vector
        def _(e):
            run("dve", e)

        @block.gpsimd
        def _(e):
            run("pool", e)

        @block.sync
        def _(e):
            run("sp", e)


class Buf:
    __slots__ = ("ap", "r")

    def __init__(self, ap, name):
        self.ap = ap
        self.r = Res(name)


class Arena:
    def __init__(self, nc, st, nbytes):
        self.t = st.enter_context(nc.sbuf_tensor("arena", [128, nbytes], U8))
        self.n = nbytes
        self.off = 0

    def mark(self):
        return self.off

    def release(self, m):
        self.off = m

    def alloc(self, name, shape, dt):
        n = int(np.prod(shape)) * DSZ[dt]
        n_al = (n + 63) // 64 * 64
        assert self.off + n_al <= self.n, f"arena overflow at {name}: {self.off}+{n_al}>{self.n}"
        ap = self.t[:, self.off:self.off + n].bitcast(dt)
        self.off += n_al
        if len(shape) == 2:
            ap = ap.rearrange("p (a b) -> p a b", a=shape[0])
        elif len(shape) == 3:
            ap = ap.rearrange("p (a b c) -> p a b c", a=shape[0], b=shape[1])
        return Buf(ap, name)


OFF_CQ, OFF_CKV, OFF_KR, OFF_XR, OFF_YG, IN_COLS = 0, 512, 768, 832, 1856, 2880
WIN_W = IN_COLS + 64
WUQ_W = 1536 + 512

V_MIXG, V_FFNG, V_QNG, V_KVNG, V_AOG, V_ROG = 0, 16, 32, 36, 38, 46
V_QHG, V_KHG, V_CW, V_CB, V_BRG, V_BIG, V_LAM = 54, 55, 56, 88, 96, 104, 112
NV128 = 120
W_QHG, W_QHGS, W_KHG, W_KHGS, W_INVF, W_SIGN = 0, 1, 2, 3, 4, 5
NV64 = 6


def build(NOWN, NPRE, NE=16384, debug=()):
    nc = bass.Bass("TRN2", target_bir_lowering=False)
    NT = NOWN + NPRE
    NGP, NGO = NPRE // 512, NOWN // 512
    NG = NGP + NGO
    NI1 = NE // 128

    def din(name, shape, dt=F32):
        return nc.dram_tensor(name, list(shape), dt, kind="ExternalInput").ap()

    def dscr(name, shape, dt):
        kind = "ExternalOutput" if name in debug else "Internal"
        return nc.dram_tensor(name, list(shape), dt, kind=kind).ap()

    x_own = din("x_own", [NOWN, D])
    x_pre = din("x_pre", [NPRE, D])
    pos_all = din("pos_all", [1, NT], I32)
    flags = din("flags", [128, 2])
    vec128 = din("vec128", [128, NV128])
    vec64 = din("vec64", [64, NV64])
    ffng_row = din("ffng_row", [1, D])
    w_in = din("w_in", [D, IN_COLS])
    w_uq = din("w_uq", [512, 1536])
    w_ukv = din("w_ukv", [256, 2048])
    w_out = din("w_out", [D, D])
    w_q = din("w_q", [D, D])
    w_rg = din("w_rg", [8, 128, 128])
    w_ig = din("w_ig", [8, 128, 128])
    keys1 = din("keys1", [8, 128, 128])
    keys2 = din("keys2", [8, 128, 128])
    u_tab = din("u_tab", [NE, D])
    v_tab = din("v_tab", [NE, D])
    out_d = nc.dram_tensor("out", [NOWN, D], F32, kind="ExternalOutput").ap()

    Kn_d = dscr("Kn_d", [8, 128, NT], BF16)
    Kr_d = dscr("Kr_d", [8, 64, NT], BF16)
    V_d = dscr("V_d", [NT, 1024], BF16)
    Qn_d = dscr("Qn_d", [8, 128, NOWN], BF16)
    Qr_d = dscr("Qr_d", [8, 64, NOWN], BF16)
    Rec_d = dscr("Rec_d", [8, 128, NOWN], BF16)
    X1_d = dscr("X1_d", [NOWN, D], F32)
    XnT_d = dscr("XnT_d", [16, 128, NOWN], BF16)
    S_d = dscr("S_d", [NOWN, 2048], F32)
    G_d = dscr("G_d", [NOWN // 128, 128, NI1 * 128], BF16)
    uT_d = dscr("uT_d", [NE // 512, 128, 16 * 512], BF16)
    vb_d = dscr("vb_d", [NE, D], BF16)

    st = ExitStack()
    P = Prog(nc)
    A = Arena(nc, st, 206 * 1024)
    psum_all = st.enter_context(nc.psum_tensor("psum_all", [128, 4096], F32))
    PB = [Buf(psum_all[:, b * 512:(b + 1) * 512], f"psb{b}") for b in range(8)]
    for _b in PB:
        _b.r.excl = True

    def ps_bf(b0, nb):
        return psum_all[:, b0 * 512:(b0 + nb) * 512].bitcast(BF16)

    def dma(q, out, in_, reads, writes):
        return P.add(q, lambda e: e.dma_start(out=out, in_=in_), reads, writes, dma=True)

    def mm(out, lhsT, rhs, start, stop, reads, writes):
        return P.add("pe", lambda e: e.matmul(out, lhsT, rhs, start=start, stop=stop), reads, writes)

    def tr(out, in_, ident, reads, writes):
        return P.add("pe", lambda e: e.transpose(out, in_, ident), reads, writes)

    def act(out, in_, func, reads, writes, bias=None, scale=None, accum=None):
        kw = {}
        if bias is not None:
            kw["bias"] = bias
        if scale is not None:
            kw["scale"] = scale
        if accum is not None:
            kw["accum_out"] = accum
        return P.add("act", lambda e: e.activation(out=out, in_=in_, func=func, **kw), reads, writes)

    def tt(eng, out, in0, in1, op, reads, writes):
        return P.add(eng, lambda e: e.tensor_tensor(out=out, in0=in0, in1=in1, op=op), reads, writes)

    def ts(eng, out, in0, s1, s2, op0, op1, reads, writes):
        if s2 is None:
            return P.add(eng, lambda e: e.tensor_scalar(out=out, in0=in0, scalar1=s1, scalar2=None, op0=op0),
                         reads, writes)
        return P.add(eng, lambda e: e.tensor_scalar(out=out, in0=in0, scalar1=s1, scalar2=s2, op0=op0, op1=op1),
                     reads, writes)

    def stt(eng, out, in0, scalar, in1, op0, op1, reads, writes):
        return P.add(eng, lambda e: e.scalar_tensor_tensor(out=out, in0=in0, scalar=scalar, in1=in1, op0=op0, op1=op1),
                     reads, writes)

    def cp(eng, out, in_, reads, writes):
        if eng == "act":
            return act(out, in_, AF.Copy, reads, writes)
        return P.add(eng, lambda e: e.tensor_copy(out=out, in_=in_), reads, writes)

    def wcast(i, out, in_, g, reads, writes):
        if i % 2 == 0:
            return ts("dve", out, in_, g, None, ALU.mult, None, reads, writes)
        return P.add("act", lambda e: e.mul(out=out, in_=in_, mul=g), reads, writes)

    def recip(out, in_, reads, writes):
        return P.add("dve", lambda e: e.reciprocal(out=out, in_=in_), reads, writes)

    def rstd_from_ps(dst, src_ap, src_res, n):
        ts("dve", dst.ap, src_ap, 1.0 / n, EPS, ALU.mult, ALU.add, src_res, [dst.r])
        act(dst.ap, dst.ap, AF.Ln, [dst.r], [dst.r])
        act(dst.ap, dst.ap, AF.Exp, [dst.r], [dst.r], scale=-0.5)

    def dbgdump(name, ap, res, shape, dt=F32):
        if name in debug:
            d_ = nc.dram_tensor(name, list(shape), dt, kind="ExternalOutput").ap()
            dma("pool", d_, ap, [res], [])

    ident = A.alloc("ident", [128], BF16)
    ones = A.alloc("ones", [128], BF16)
    iota_f = A.alloc("iota_f", [128], F32)
    tri = A.alloc("tri", [128], BF16)
    bmask = A.alloc("bmask", [8], BF16)
    v128 = A.alloc("v128", [NV128], F32)
    v64 = A.alloc("v64", [NV64], F32)
    flg = A.alloc("flg", [2], F32)
    cL = A.alloc("cL", [8], F32)
    gsc = A.alloc("gsc", [8], F32)
    tmpc = A.alloc("tmpc", [128], F32)

    dma("sp", v128.ap, vec128, [], [v128.r])
    dma("sp", v64.ap[0:64], vec64, [], [v64.r])
    dma("sp", flg.ap, flags, [], [flg.r])
    P.add("pool", lambda e: e.iota(iota_f.ap, pattern=[[1, 128]], base=0, channel_multiplier=0,
                                   allow_small_or_imprecise_dtypes=True), [], [iota_f.r])
    P.add("pool", lambda e: e.iota(tmpc.ap, pattern=[[1, 128]], base=0, channel_multiplier=-1,
                                   allow_small_or_imprecise_dtypes=True), [], [tmpc.r])
    P.add("dve", lambda e: e.tensor_single_scalar(out=ident.ap, in_=tmpc.ap, scalar=0.0, op=ALU.is_equal),
          [tmpc.r], [ident.r])
    P.add("dve", lambda e: e.tensor_single_scalar(out=tri.ap, in_=tmpc.ap, scalar=0.0, op=ALU.is_ge),
          [tmpc.r], [tri.r])
    P.add("dve", lambda e: e.memset(ones.ap, 1.0), [], [ones.r])
    P.add("pool", lambda e: e.iota(tmpc.ap[:, 0:8], pattern=[[-16, 8]], base=0, channel_multiplier=1,
                                   allow_small_or_imprecise_dtypes=True), [tmpc.r], [tmpc.r])
    P.add("dve", lambda e: e.tensor_single_scalar(out=tmpc.ap[:, 8:16], in_=tmpc.ap[:, 0:8], scalar=0.0, op=ALU.is_ge),
          [tmpc.r], [tmpc.r])
    P.add("dve", lambda e: e.tensor_single_scalar(out=tmpc.ap[:, 16:24], in_=tmpc.ap[:, 0:8], scalar=15.0, op=ALU.is_le),
          [tmpc.r], [tmpc.r])
    tt("dve", bmask.ap, tmpc.ap[:, 8:16], tmpc.ap[:, 16:24], ALU.mult, [tmpc.r], [bmask.r])
    act(cL.ap, v128.ap[:, V_LAM:V_LAM + 8], AF.Exp, [v128.r], [cL.r], scale=-1.0)
    act(cL.ap, cL.ap, AF.Ln, [cL.r], [cL.r], bias=1.0)
    ts("dve", cL.ap, cL.ap, -8.0, None, ALU.mult, None, [cL.r], [cL.r])
    SC = 192.0 ** -0.5
    ts("dve", gsc.ap[:, 0:1], v128.ap[:, V_QHG:V_QHG + 1], SC, None, ALU.mult, None, [v128.r], [gsc.r])
    ts("dve", gsc.ap[0:64, 1:2], v64.ap[0:64, W_QHG:W_QHG + 1], SC, None, ALU.mult, None, [v64.r], [gsc.r])
    ts("dve", gsc.ap[0:64, 2:3], v64.ap[0:64, W_QHGS:W_QHGS + 1], v64.ap[0:64, W_SIGN:W_SIGN + 1], SC,
       ALU.mult, ALU.mult, [v64.r], [gsc.r])
    ts("dve", gsc.ap[0:64, 3:4], v64.ap[0:64, W_KHGS:W_KHGS + 1], v64.ap[0:64, W_SIGN:W_SIGN + 1], None,
       ALU.mult, None, [v64.r], [gsc.r])

    m_const = A.mark()

    def phase_tables():
        grow = A.alloc("grow", [D], F32)
        dma("sp", grow.ap, ffng_row.partition_broadcast(128), [], [grow.r])
        TE = 4
        ust = [A.alloc("ust0", [TE, D], F32)]
        ub = [A.alloc("ub0", [TE, D], BF16)]
        uTs = [A.alloc(f"uTs{i}", [16, TE * 128], BF16) for i in range(2)]
        vst = [A.alloc("vst0", [TE, D], F32)]
        vb = [A.alloc("vb0", [TE, D], BF16)]
        for it in range(NE // (TE * 128)):
            e0 = it * TE * 128
            UTS = uTs[it % 2]
            dma("sp", ust[0].ap, u_tab[e0:e0 + TE * 128, :].rearrange("(a p) d -> p a d", p=128), [], [ust[0].r])
            dma("sp", vst[0].ap, v_tab[e0:e0 + TE * 128, :].rearrange("(a p) d -> p a d", p=128), [], [vst[0].r])
            tt("dve", ub[0].ap, ust[0].ap, grow.ap[:, None, :].to_broadcast([128, TE, D]), ALU.mult,
               [ust[0].r, grow.r], [ub[0].r])
            for et in range(TE):
                b0 = (et % 4) * 2
                pv = ps_bf(b0, 2)
                for c in range(16):
                    tr(pv[:, c * 128:(c + 1) * 128], ub[0].ap[:, et, c * 128:(c + 1) * 128], ident.ap,
                       [ub[0].r, ident.r], [PB[b0].r, PB[b0 + 1].r])
                cp("act", UTS.ap[:, :, et * 128:(et + 1) * 128],
                   pv.rearrange("p (c e) -> p c e", c=16), [PB[b0].r, PB[b0 + 1].r], [UTS.r])
            dma("pool", uT_d[it], UTS.ap.rearrange("p c e -> p (c e)"), [UTS.r], [])
            cp("dve", vb[0].ap, vst[0].ap, [vst[0].r], [vb[0].r])
            dma("pool", vb_d[e0:e0 + TE * 128, :].rearrange("(a p) d -> p a d", p=128), vb[0].ap, [vb[0].r], [])

    def phase1():
        G1 = 256
        T1 = G1 // 128
        w_in_bf = A.alloc("w_in_bf", [16, WIN_W], BF16)
        w_uq_bf = A.alloc("w_uq_bf", [4, WUQ_W], BF16)
        w_ukv_bf = A.alloc("w_ukv_bf", [2, 2048], BF16)
        w_rg_bf = A.alloc("w_rg_bf", [8, 128], BF16)
        w_ig_bf = A.alloc("w_ig_bf", [8, 128], BF16)
        m1 = A.mark()
        stg = [A.alloc(f"stg{i}", [IN_COLS], F32) for i in range(2)]
        for c in range(16):
            s = stg[c % 2]
            g = v128.ap[:, V_MIXG + c:V_MIXG + c + 1]
            dma("sp", s.ap, w_in[c * 128:(c + 1) * 128, :], [], [s.r])
            wcast(c, w_in_bf.ap[:, c, 0:IN_COLS], s.ap, g, [s.r, v128.r], [w_in_bf.r])
            ts("dve", w_in_bf.ap[:, c, IN_COLS:IN_COLS + 32], s.ap[:, OFF_KR + 32:OFF_KR + 64], g, None, ALU.mult, None,
               [s.r, v128.r], [w_in_bf.r])
            ts("dve", w_in_bf.ap[:, c, IN_COLS + 32:IN_COLS + 64], s.ap[:, OFF_KR:OFF_KR + 32], g, None, ALU.mult, None,
               [s.r, v128.r], [w_in_bf.r])
        for c in range(4):
            s = stg[c % 2]
            g = v128.ap[:, V_QNG + c:V_QNG + c + 1]
            dma("sp", s.ap[:, 0:1536], w_uq[c * 128:(c + 1) * 128, :], [], [s.r])
            ts("dve", w_uq_bf.ap[:, c, 0:1536], s.ap[:, 0:1536], g, None, ALU.mult, None, [s.r, v128.r], [w_uq_bf.r])
            sv = s.ap[:, 0:1536].rearrange("p (h k) -> p h k", h=8)
            dv = w_uq_bf.ap[:, c, 1536:2048].rearrange("p (h k) -> p h k", h=8)
            ts("dve", dv[:, :, 0:32], sv[:, :, 160:192], g, None, ALU.mult, None, [s.r, v128.r], [w_uq_bf.r])
            ts("dve", dv[:, :, 32:64], sv[:, :, 128:160], g, None, ALU.mult, None, [s.r, v128.r], [w_uq_bf.r])
        for c in range(2):
            s = stg[c % 2]
            g = v128.ap[:, V_KVNG + c:V_KVNG + c + 1]
            dma("sp", s.ap[:, 0:2048], w_ukv[c * 128:(c + 1) * 128, :], [], [s.r])
            ts("dve", w_ukv_bf.ap[:, c, :], s.ap[:, 0:2048], g, None, ALU.mult, None, [s.r, v128.r], [w_ukv_bf.r])
        for (wsrc, wdst) in ((w_rg, w_rg_bf), (w_ig, w_ig_bf)):
            s = stg[0]
            dma("sp", s.ap[:, 0:1024].rearrange("p (h j) -> p h j", h=8), wsrc.rearrange("h i j -> i h j"), [], [s.r])
            cp("dve", wdst.ap, s.ap[:, 0:1024].rearrange("p (h j) -> p h j", h=8), [s.r], [wdst.r])
        P.barrier()
        A.release(m1)

        def al(name, shape, dt, n=1):
            return [A.alloc(f"{name}{i}", shape, dt) for i in range(n)]
        xt = al("xt", [D], F32, 1)
        ss1 = al("ss1_", [1], F32, 2)
        xb = al("xb", [D], BF16, 1)
        hT = al("hT", [16, G1], BF16, 1)
        cq_f = A.alloc("cq_f", [4, G1], F32)
        ckv_f = A.alloc("ckv_f", [2, G1], F32)
        cqn = A.alloc("cqn", [4, G1], BF16)
        ckvn = A.alloc("ckvn", [2, G1], BF16)
        sqb = al("sqb", [G1], BF16, 3)
        sqr = al("sqr", [G1], BF16, 2)
        rs = al("rs", [G1], F32, 3)
        kr_f = A.alloc("kr_f", [G1], F32)
        krs_f = A.alloc("krs_f", [G1], F32)
        kro = A.alloc("kro", [G1], F32)
        sqkr = A.alloc("sqkr", [G1], BF16)
        posi = A.alloc("posi", [G1], I32)
        ang = A.alloc("ang", [G1], F32)
        ang2 = A.alloc("ang2", [G1], F32)
        tq = A.alloc("tq", [G1], F32)
        CC = A.alloc("CC", [G1], F32)
        SS = A.alloc("SS", [G1], F32)
        tA = al("tA", [G1], F32, 1)
        tB = al("tB", [G1], F32, 1)
        qn_o = al("qn_o", [G1], BF16, 2)
        qr_o = al("qr_o", [G1], BF16, 2)
        kn_o = al("kn_o", [G1], BF16, 2)
        kr_o = al("kr_o", [G1], BF16, 2)
        v_o = al("v_o", [1024], BF16, 1)
        xrf = al("xrf", [G1 + 3], F32, 2)
        halo = A.alloc("halo", [8, 3], F32)
        hstate = A.alloc("hstate", [8], F32)
        c0 = al("c0_", [G1], F32, 2)
        xc = al("xc", [G1], F32, 2)
        xcb = al("xcb", [G1], BF16, 2)
        rg = al("rg", [G1], F32, 2)
        ig = al("ig", [G1], F32, 2)
        av = al("av", [G1], F32, 2)
        a2 = al("a2", [G1], F32, 2)
        hs = al("hs", [G1], F32, 2)
        gy = al("gy", [G1], F32, 2)
        rec_f = A.alloc("rec_f", [8, G1], F32)
        rec_n = al("rec_n", [8, G1], BF16, 1)

        P.add("dve", lambda e: e.memset(halo.ap, 0.0), [], [halo.r])
        P.add("dve", lambda e: e.memset(hstate.ap, 0.0), [], [hstate.r])

        bank_rr = [2]

        def nb():
            b = bank_rr[0]
            bank_rr[0] = 2 + (b - 2 + 1) % 5
            return PB[b]

        cnt = {"t": 0, "sq": 0, "rs": 0, "h": 0, "o": 0}
        NG1 = NT // G1
        NGP1 = NPRE // G1
        import os as _os
        _lim = int(_os.environ.get("LIM", "999"))
        _sub = _os.environ.get("SUB", "")
        for gi in range(NG1):
            if gi >= _lim:
                break
            own = gi >= NGP1
            go = gi - NGP1
            xsrc = x_own if own else x_pre
            t0g = (go if own else gi) * G1
            tg_all = gi * G1
            hTg = hT[0]
            for t4 in range(T1):
                k = cnt["t"]
                cnt["t"] += 1
                X, XB_, S1 = xt[0], xb[0], ss1[k % 2]
                dma("sp", X.ap, xsrc[t0g + t4 * 128:t0g + (t4 + 1) * 128, :], [], [X.r])
                P.add("pool", lambda e, S1=S1: e.memset(S1.ap, 0.0), [], [S1.r])
                act(XB_.ap, X.ap, AF.Square, [X.r, S1.r], [XB_.r, S1.r], accum=S1.ap)
                rstd_from_ps(S1, S1.ap, [S1.r], D)
                ts("dve", XB_.ap, X.ap, S1.ap[:, 0:1], None, ALU.mult, None, [X.r, S1.r], [XB_.r])
                pv = ps_bf(0, 2)
                for c in range(16):
                    tr(pv[:, c * 128:(c + 1) * 128], XB_.ap[:, c * 128:(c + 1) * 128], ident.ap,
                       [XB_.r, ident.r], [PB[0].r, PB[1].r])
                cp("act" if t4 % 2 == 0 else "dve", hTg.ap[:, :, t4 * 128:(t4 + 1) * 128],
                   pv.rearrange("p (c t) -> p c t", c=16), [PB[0].r, PB[1].r], [hTg.r])

            def proj(col0, M):
                pb = nb()
                for c in range(16):
                    mm(pb.ap[0:M, 0:G1], w_in_bf.ap[:, c, col0:col0 + M], hTg.ap[:, c, :], c == 0, c == 15,
                       [w_in_bf.r, hTg.r], [pb.r])
                return pb

            def norm_chunks(col0, nch, dst_f, dst_n, n):
                pss = nb()
                for c in range(nch):
                    pb = proj(col0 + c * 128, 128)
                    sq = sqb[cnt["sq"] % 3]
                    cnt["sq"] += 1
                    cp("dve", dst_f.ap[:, c, :], pb.ap[:, 0:G1], [pb.r], [dst_f.r])
                    act(sq.ap, dst_f.ap[:, c, :], AF.Square, [dst_f.r], [sq.r])
                    mm(pss.ap[:, 0:G1], ones.ap, sq.ap, c == 0, c == nch - 1, [ones.r, sq.r], [pss.r])
                r = rs[cnt["rs"] % 3]
                cnt["rs"] += 1
                rstd_from_ps(r, pss.ap[:, 0:G1], [pss.r], n)
                tt("dve", dst_n.ap, dst_f.ap, r.ap[:, None, :].to_broadcast([128, nch, G1]), ALU.mult,
                   [dst_f.r, r.r], [dst_n.r])

            if _sub == "a":
                continue
            _sb = int(_os.environ.get("SUBB", "99"))
            dma("sp", posi.ap[0:64], pos_all[:, tg_all:tg_all + G1].partition_broadcast(64), [], [posi.r])
            cp("dve", ang.ap[0:64], posi.ap[0:64], [posi.r], [ang.r])
            if _sb >= 1:
                ts("dve", ang.ap[0:64], ang.ap[0:64], v64.ap[0:64, W_INVF:W_INVF + 1], None, ALU.mult, None,
                   [ang.r, v64.r], [ang.r])
                ts("dve", ang2.ap[0:64], ang.ap[0:64], float(np.pi / 2), None, ALU.add, None, [ang.r], [ang2.r])
            for (src, dst) in ((ang, SS), (ang2, CC)):
                if _sb >= 2:
                    ts("dve", tq.ap[0:64], src.ap[0:64], float(1.0 / (2 * np.pi)), None, ALU.mult, None, [src.r], [tq.r])
                    cp("dve", posi.ap[0:64], tq.ap[0:64], [tq.r], [posi.r])
                    cp("dve", tq.ap[0:64], posi.ap[0:64], [posi.r], [tq.r])
                if _sb >= 3:
                    stt("dve", tq.ap[0:64], tq.ap[0:64], float(-2 * np.pi), src.ap[0:64], ALU.mult, ALU.add,
                        [tq.r, src.r], [tq.r])
                if _sb >= 4:
                    zz = tA[0]
                    qq = tB[0]
                    tt("dve", zz.ap[0:64], tq.ap[0:64], tq.ap[0:64], ALU.mult, [tq.r], [zz.r])
                    cs = [-1.0 / 6, 1.0 / 120, -1.0 / 5040, 1.0 / 362880, -1.0 / 39916800, 1.0 / 6227020800,
                          -1.0 / 1307674368000]
                    ts("dve", qq.ap[0:64], zz.ap[0:64], cs[6], None, ALU.mult, None, [zz.r], [qq.r])
                    for kk in range(5, -1, -1):
                        stt("dve", qq.ap[0:64], qq.ap[0:64], cs[kk], zz.ap[0:64], ALU.add, ALU.mult,
                            [qq.r, zz.r], [qq.r])
                    stt("dve", dst.ap[0:64], qq.ap[0:64], 1.0, tq.ap[0:64], ALU.add, ALU.mult, [qq.r, tq.r], [dst.r])
            if _sub == "b":
                continue
            norm_chunks(OFF_CKV, 2, ckv_f, ckvn, 256)
            pb = proj(OFF_KR, 64)
            cp("act", kr_f.ap[0:64], pb.ap[0:64, 0:G1], [pb.r], [kr_f.r])
            pb = proj(IN_COLS, 64)
            cp("act", krs_f.ap[0:64], pb.ap[0:64, 0:G1], [pb.r], [krs_f.r])
            act(sqkr.ap[0:64], kr_f.ap[0:64], AF.Square, [kr_f.r], [sqkr.r])
            stt("dve", tA[0].ap[0:64], kr_f.ap[0:64], v64.ap[0:64, W_KHG:W_KHG + 1], CC.ap[0:64], ALU.mult, ALU.mult,
                [kr_f.r, v64.r, CC.r], [tA[0].r])
            stt("dve", tB[0].ap[0:64], krs_f.ap[0:64], gsc.ap[0:64, 3:4], SS.ap[0:64], ALU.mult, ALU.mult,
                [krs_f.r, gsc.r, SS.r], [tB[0].r])
            tt("dve", kro.ap[0:64], tA[0].ap[0:64], tB[0].ap[0:64], ALU.add, [tA[0].r, tB[0].r], [kro.r])
            for h in range(8):
                KN, KR = kn_o[h % 2], kr_o[h % 2]
                pk = nb()
                for c in range(2):
                    mm(pk.ap[:, 0:G1], w_ukv_bf.ap[:, c, h * 256:h * 256 + 128], ckvn.ap[:, c, :], c == 0, c == 1,
                       [w_ukv_bf.r, ckvn.r], [pk.r])
                sq = sqb[cnt["sq"] % 3]
                cnt["sq"] += 1
                act(sq.ap, pk.ap[:, 0:G1], AF.Square, [pk.r], [sq.r])
                pss = nb()
                mm(pss.ap[:, 0:G1], ones.ap, sq.ap, True, False, [ones.r, sq.r], [pss.r])
                mm(pss.ap[:, 0:G1], ones.ap[0:64, :], sqkr.ap[0:64], False, True, [ones.r, sqkr.r], [pss.r])
                r = rs[cnt["rs"] % 3]
                cnt["rs"] += 1
                rstd_from_ps(r, pss.ap[:, 0:G1], [pss.r], 192)
                stt("dve", KN.ap, pk.ap[:, 0:G1], v128.ap[:, V_KHG:V_KHG + 1], r.ap, ALU.mult, ALU.mult,
                    [pk.r, v128.r, r.r], [KN.r])
                tt("dve", KR.ap[0:64], kro.ap[0:64], r.ap[0:64], ALU.mult, [kro.r, r.r], [KR.r])
                dma("pool", Kn_d[h, :, tg_all:tg_all + G1], KN.ap, [KN.r], [])
                dma("pool", Kr_d[h, :, tg_all:tg_all + G1], KR.ap[0:64], [KR.r], [])
            if _sub == "c":
                continue
            wv = [w_ukv_bf.ap[:, c, :].rearrange("p (h k) -> p h k", h=8) for c in range(2)]
            for t4 in range(T1):
                VO = v_o[0]
                cnt["o"] += 1
                for n2 in range(2):
                    pvv = nb()
                    for c in range(2):
                        mm(pvv.ap.rearrange("p (h k) -> p h k", h=4), ckvn.ap[:, c, t4 * 128:(t4 + 1) * 128],
                           wv[c][:, n2 * 4:(n2 + 1) * 4, 128:256], c == 0, c == 1, [ckvn.r, w_ukv_bf.r], [pvv.r])
                    cp("act", VO.ap[:, n2 * 512:(n2 + 1) * 512], pvv.ap, [pvv.r], [VO.r])
                r0 = tg_all + t4 * 128
                dma("pool", V_d[r0:r0 + 128, :], VO.ap, [VO.r], [])

            if _sub == "d":
                continue
            if own:
                norm_chunks(OFF_CQ, 4, cq_f, cqn, 512)
                _qs = int(_os.environ.get("QS", "99"))
                for h in range(8 if _qs > 0 else 0):
                    QN, QR = qn_o[h % 2], qr_o[h % 2]
                    pqn, pqr, pqs = nb(), nb(), nb()
                    for (pb_, col, M) in ((pqn, h * 192, 128), (pqr, h * 192 + 128, 64), (pqs, 1536 + h * 64, 64)):
                        for c in range(4):
                            mm(pb_.ap[0:M, 0:G1], w_uq_bf.ap[:, c, col:col + M], cqn.ap[:, c, :], c == 0, c == 3,
                               [w_uq_bf.r, cqn.r], [pb_.r])
                    sq = sqb[cnt["sq"] % 3]
                    cnt["sq"] += 1
                    sq2 = sqr[h % 2]
                    act(sq.ap, pqn.ap[:, 0:G1], AF.Square, [pqn.r], [sq.r])
                    act(sq2.ap[0:64], pqr.ap[0:64, 0:G1], AF.Square, [pqr.r], [sq2.r])
                    pss = nb()
                    mm(pss.ap[:, 0:G1], ones.ap, sq.ap, True, False, [ones.r, sq.r], [pss.r])
                    mm(pss.ap[:, 0:G1], ones.ap[0:64, :], sq2.ap[0:64], False, True, [ones.r, sq2.r], [pss.r])
                    r = rs[cnt["rs"] % 3]
                    cnt["rs"] += 1
                    rstd_from_ps(r, pss.ap[:, 0:G1], [pss.r], 192)
                    stt("dve", QN.ap, pqn.ap[:, 0:G1], gsc.ap[:, 0:1], r.ap, ALU.mult, ALU.mult,
                        [pqn.r, gsc.r, r.r], [QN.r])
                    TA, TB = tA[0], tB[0]
                    stt("dve", TA.ap[0:64], pqr.ap[0:64, 0:G1], gsc.ap[0:64, 1:2], r.ap[0:64], ALU.mult, ALU.mult,
                        [pqr.r, gsc.r, r.r], [TA.r])
                    stt("dve", TB.ap[0:64], pqs.ap[0:64, 0:G1], gsc.ap[0:64, 2:3], r.ap[0:64], ALU.mult, ALU.mult,
                        [pqs.r, gsc.r, r.r], [TB.r])
                    tt("dve", TA.ap[0:64], TA.ap[0:64], CC.ap[0:64], ALU.mult, [TA.r, CC.r], [TA.r])
                    tt("dve", TB.ap[0:64], TB.ap[0:64], SS.ap[0:64], ALU.mult, [TB.r, SS.r], [TB.r])
                    tt("dve", QR.ap[0:64], TA.ap[0:64], TB.ap[0:64], ALU.add, [TA.r, TB.r], [QR.r])
                    dma("pool", Qn_d[h, :, t0g:t0g + G1], QN.ap, [QN.r], [])
                    dma("pool", Qr_d[h, :, t0g:t0g + G1], QR.ap[0:64], [QR.r], [])

            if _sub == "e":
                continue
            if own and go == 0:
                ts("dve", hstate.ap, hstate.ap, flg.ap[:, 0:1], None, ALU.mult, None, [hstate.r, flg.r], [hstate.r])
                ts("dve", halo.ap, halo.ap, flg.ap[:, 0:1], None, ALU.mult, None, [halo.r, flg.r], [halo.r])
            pss_rec = PB[7]
            _rs = int(_os.environ.get("RS", "999"))
            for h in range(8):
                k = cnt["h"]
                cnt["h"] += 1
                s2 = k % 2
                XR = xrf[s2]
                if 1 <= _rs:
                    pb = proj(OFF_XR + h * 128, 128)
                if 2 <= _rs:
                    cp("dve", XR.ap[:, 0:3], halo.ap[:, h, :], [halo.r], [XR.r])
                if 3 <= _rs:
                    cp("act", XR.ap[:, 3:G1 + 3], pb.ap[:, 0:G1], [pb.r], [XR.r])
                if 4 <= _rs:
                    cp("dve", halo.ap[:, h, :], XR.ap[:, G1:G1 + 3], [XR.r], [halo.r])
                cw = lambda j: v128.ap[:, V_CW + j * 8 + h:V_CW + j * 8 + h + 1]
                if 5 <= _rs:
                    ts("dve", c0[s2].ap, XR.ap[:, 0:G1], cw(0), v128.ap[:, V_CB + h:V_CB + h + 1], ALU.mult, ALU.add,
                       [XR.r, v128.r], [c0[s2].r])
                if 6 <= _rs:
                    stt("dve", c0[s2].ap, XR.ap[:, 1:G1 + 1], cw(1), c0[s2].ap, ALU.mult, ALU.add,
                        [XR.r, v128.r, c0[s2].r], [c0[s2].r])
                if 7 <= _rs:
                    stt("dve", c0[s2].ap, XR.ap[:, 2:G1 + 2], cw(2), c0[s2].ap, ALU.mult, ALU.add,
                        [XR.r, v128.r, c0[s2].r], [c0[s2].r])
                if 8 <= _rs:
                    stt("dve", xc[s2].ap, XR.ap[:, 3:G1 + 3], cw(3), c0[s2].ap, ALU.mult, ALU.add,
                        [XR.r, v128.r, c0[s2].r], [xc[s2].r])
                if 9 <= _rs:
                    cp("act", xcb[s2].ap, xc[s2].ap, [xc[s2].r], [xcb[s2].r])
                pr, pi = nb(), nb()
                if 10 <= _rs:
                    mm(pr.ap[:, 0:G1], w_rg_bf.ap[:, h, :], xcb[s2].ap, True, True, [w_rg_bf.r, xcb[s2].r], [pr.r])
                if 11 <= _rs:
                    mm(pi.ap[:, 0:G1], w_ig_bf.ap[:, h, :], xcb[s2].ap, True, True, [w_ig_bf.r, xcb[s2].r], [pi.r])
                if 12 <= _rs:
                    act(rg[s2].ap, pr.ap[:, 0:G1], AF.Sigmoid, [pr.r, v128.r], [rg[s2].r],
                        bias=v128.ap[:, V_BRG + h:V_BRG + h + 1])
                if 13 <= _rs:
                    act(ig[s2].ap, pi.ap[:, 0:G1], AF.Sigmoid, [pi.r, v128.r], [ig[s2].r],
                        bias=v128.ap[:, V_BIG + h:V_BIG + h + 1])
                if 14 <= _rs:
                    ts("dve", av[s2].ap, rg[s2].ap, cL.ap[:, h:h + 1], None, ALU.mult, None, [rg[s2].r, cL.r], [av[s2].r])
                if 15 <= _rs:
                    act(av[s2].ap, av[s2].ap, AF.Exp, [av[s2].r], [av[s2].r])
                if 16 <= _rs:
                    tt("dve", a2[s2].ap, av[s2].ap, av[s2].ap, ALU.mult, [av[s2].r], [a2[s2].r])
                if 17 <= _rs:
                    act(a2[s2].ap, a2[s2].ap, AF.Ln, [a2[s2].r], [a2[s2].r], scale=-1.0, bias=1.0)
                    act(a2[s2].ap, a2[s2].ap, AF.Exp, [a2[s2].r], [a2[s2].r], scale=0.5)
                if 18 <= _rs:
                    tt("dve", ig[s2].ap, ig[s2].ap, xc[s2].ap, ALU.mult, [ig[s2].r, xc[s2].r], [ig[s2].r])
                if 19 <= _rs:
                    tt("dve", ig[s2].ap, ig[s2].ap, a2[s2].ap, ALU.mult, [ig[s2].r, a2[s2].r], [ig[s2].r])
                if 20 <= _rs:
                    P.add("dve", lambda e, s2=s2, h=h: e.tensor_tensor_scan(
                        out=hs[s2].ap, data0=av[s2].ap, data1=ig[s2].ap, initial=hstate.ap[:, h:h + 1],
                        op0=ALU.mult, op1=ALU.add), [av[s2].r, ig[s2].r, hstate.r], [hs[s2].r])
                if 21 <= _rs:
                    cp("act", hstate.ap[:, h:h + 1], hs[s2].ap[:, G1 - 1:G1], [hs[s2].r], [hstate.r])
                if gi == NGP1 and h == 0:
                    dbgdump("d_xc", xc[s2].ap, xc[s2].r, [128, G1])
                    dbgdump("d_rg", rg[s2].ap, rg[s2].r, [128, G1])
                    dbgdump("d_b", ig[s2].ap, ig[s2].r, [128, G1])
                    dbgdump("d_a", av[s2].ap, av[s2].r, [128, G1])
                    dbgdump("d_hs", hs[s2].ap, hs[s2].r, [128, G1])
                    dbgdump("d_xr", XR.ap, XR.r, [128, G1 + 3])
                if own:
                    pb = proj(OFF_YG + h * 128, 128)
                    act(gy[s2].ap, pb.ap[:, 0:G1], AF.Gelu, [pb.r], [gy[s2].r])
                    if gi == NGP1 and h == 0:
                        dbgdump("d_gy", gy[s2].ap, gy[s2].r, [128, G1])
                    tt("dve", rec_f.ap[:, h, :], gy[s2].ap, hs[s2].ap, ALU.mult, [gy[s2].r, hs[s2].r], [rec_f.r])
                    sq = sqb[cnt["sq"] % 3]
                    cnt["sq"] += 1
                    act(sq.ap, rec_f.ap[:, h, :], AF.Square, [rec_f.r], [sq.r])
                    mm(pss_rec.ap[:, 0:G1], ones.ap, sq.ap, h == 0, h == 7, [ones.r, sq.r], [pss_rec.r])
            if own:
                r = rs[cnt["rs"] % 3]
                cnt["rs"] += 1
                rstd_from_ps(r, pss_rec.ap[:, 0:G1], [pss_rec.r], 1024)
                RN = rec_n[0]
                tt("dve", RN.ap, rec_f.ap, r.ap[:, None, :].to_broadcast([128, 8, G1]), ALU.mult,
                   [rec_f.r, r.r], [RN.r])
                dma("pool", Rec_d[:, :, t0g:t0g + G1].rearrange("h p t -> p h t"), RN.ap, [RN.r], [])

    def phase23():
        w_out_bf = A.alloc("w_out_bf", [16, D], BF16)
        m1 = A.mark()
        stg = [A.alloc(f"stg{i}", [D], F32) for i in range(2)]
        for c in range(16):
            s = stg[c % 2]
            g = v128.ap[:, V_AOG + c:V_AOG + c + 1]
            dma("sp", s.ap, w_out[c * 128:(c + 1) * 128, :], [], [s.r])
            wcast(c, w_out_bf.ap[:, c, :], s.ap, g, [s.r, v128.r], [w_out_bf.r])
        P.barrier()
        A.release(m1)
        NKB = 3
        kn = [A.alloc(f"kn{i}", [512], BF16) for i in range(NKB)]
        kr = [A.alloc(f"kr{i}", [512], BF16) for i in range(NKB)]
        vv = [A.alloc(f"vv{i}", [4, 128], BF16) for i in range(NKB)]
        qn = [A.alloc(f"qn{i}", [512], BF16) for i in range(2)]
        qr = [A.alloc(f"qr{i}", [512], BF16) for i in range(2)]
        pT = [A.alloc(f"pT{i}", [512], BF16) for i in range(3)]
        rl = A.alloc("rl", [512], F32)
        attnT = A.alloc("attnT", [8, 512], F32)
        sqa = [A.alloc(f"sqa{i}", [512], BF16) for i in range(2)]
        ra = A.alloc("ra", [512], F32)
        mixT = [A.alloc(f"mixT{i}", [16, 512], BF16) for i in range(1)]
        xo = [A.alloc(f"xo{i}", [D], F32) for i in range(2)]
        x1 = [A.alloc(f"x1_{i}", [D], F32) for i in range(2)]
        junk = A.alloc("junk", [D], BF16)
        ss1 = [A.alloc(f"ss1_{i}", [1], F32) for i in range(2)]
        xb = [A.alloc(f"xb{i}", [D], BF16) for i in range(2)]
        xnTs = [A.alloc(f"xnTs{i}", [16, 512], BF16) for i in range(1)]
        kc_cnt = 0
        q_cnt = 0
        p_cnt = 0
        t_cnt = 0
        NPC = NPRE // 512
        for j in range(NGO):
            MX = mixT[0]
            dma("sp", MX.ap[:, 8:16, :], Rec_d[:, :, j * 512:(j + 1) * 512].rearrange("h p t -> p h t"), [], [MX.r])
            nchunks = NPC + j + 1
            for h in range(8):
                QN, QR = qn[q_cnt % 2], qr[q_cnt % 2]
                q_cnt += 1
                dma("sp", QN.ap, Qn_d[h, :, j * 512:(j + 1) * 512], [], [QN.r])
                dma("sp", QR.ap[0:64], Qr_d[h, :, j * 512:(j + 1) * 512], [], [QR.r])
                po, pl = (PB[2], PB[3]) if q_cnt % 2 == 0 else (PB[5], PB[6])
                for kc in range(nchunks):
                    KN, KR, VV = kn[kc_cnt % NKB], kr[kc_cnt % NKB], vv[kc_cnt % NKB]
                    kc_cnt += 1
                    k0 = kc * 512
                    dma("sp", KN.ap, Kn_d[h, :, k0:k0 + 512], [], [KN.r])
                    dma("sp", KR.ap[0:64], Kr_d[h, :, k0:k0 + 512], [], [KR.r])
                    dma("sp", VV.ap, V_d[k0:k0 + 512, h * 128:(h + 1) * 128].rearrange("(a p) d -> p a d", p=128),
                        [], [VV.r])
                    diag = kc == nchunks - 1
                    for kb in range(4):
                        qlo = kb * 128 if diag else 0
                        psb = PB[p_cnt % 2]
                        PT = pT[p_cnt % 3]
                        p_cnt += 1
                        mm(psb.ap[:, qlo:], KN.ap[:, kb * 128:(kb + 1) * 128], QN.ap[:, qlo:], True, False,
                           [KN.r, QN.r], [psb.r])
                        mm(psb.ap[:, qlo:], KR.ap[0:64, kb * 128:(kb + 1) * 128], QR.ap[0:64, qlo:], False, True,
                           [KR.r, QR.r], [psb.r])
                        if kc < NPC:
                            act(PT.ap[:, qlo:], psb.ap[:, qlo:], AF.Exp, [psb.r, flg.r], [PT.r], bias=flg.ap[:, 1:2])
                        else:
                            act(PT.ap[:, qlo:], psb.ap[:, qlo:], AF.Exp, [psb.r], [PT.r])
                        if diag:
                            tt("dve", PT.ap[:, qlo:qlo + 128], PT.ap[:, qlo:qlo + 128], tri.ap, ALU.mult,
                               [PT.r, tri.r], [PT.r])
                        first = kc == 0 and kb == 0
                        last = diag and kb == 3
                        mm(po.ap[:, qlo:], VV.ap[:, kb, :], PT.ap[:, qlo:], first, last, [VV.r, PT.r], [po.r])
                        mm(pl.ap[:, qlo:], ones.ap, PT.ap[:, qlo:], first, last, [ones.r, PT.r], [pl.r])
                recip(rl.ap, pl.ap, [pl.r], [rl.r])
                tt("dve", attnT.ap[:, h, :], po.ap, rl.ap, ALU.mult, [po.r, rl.r], [attnT.r])
                SQ = sqa[h % 2]
                act(SQ.ap, attnT.ap[:, h, :], AF.Square, [attnT.r], [SQ.r])
                mm(PB[4].ap, ones.ap, SQ.ap, h == 0, h == 7, [ones.r, SQ.r], [PB[4].r])
            rstd_from_ps(ra, PB[4].ap, [PB[4].r], 1024)
            tt("dve", MX.ap[:, 0:8, :], attnT.ap, ra.ap[:, None, :].to_broadcast([128, 8, 512]), ALU.mult,
               [attnT.r, ra.r], [MX.r])
            XN = xnTs[0]
            for t4 in range(4):
                k = t_cnt
                t_cnt += 1
                XO, X1, S1, XB_ = xo[k % 2], x1[k % 2], ss1[k % 2], xb[k % 2]
                r0 = j * 512 + t4 * 128
                dma("sp", XO.ap, x_own[r0:r0 + 128, :], [], [XO.r])
                for n4 in range(4):
                    pb = PB[n4 % 2]
                    for c in range(16):
                        mm(pb.ap, MX.ap[:, c, t4 * 128:(t4 + 1) * 128], w_out_bf.ap[:, c, n4 * 512:(n4 + 1) * 512],
                           c == 0, c == 15, [MX.r, w_out_bf.r], [pb.r])
                    tt("dve", X1.ap[:, n4 * 512:(n4 + 1) * 512], pb.ap, XO.ap[:, n4 * 512:(n4 + 1) * 512], ALU.add,
                       [pb.r, XO.r], [X1.r])
                dma("pool", X1_d[r0:r0 + 128, :], X1.ap, [X1.r], [])
                P.add("pool", lambda e, S1=S1: e.memset(S1.ap, 0.0), [], [S1.r])
                act(junk.ap, X1.ap, AF.Square, [X1.r, S1.r], [junk.r, S1.r], accum=S1.ap)
                rstd_from_ps(S1, S1.ap, [S1.r], D)
                ts("dve", XB_.ap, X1.ap, S1.ap[:, 0:1], None, ALU.mult, None, [X1.r, S1.r], [XB_.r])
                pv = ps_bf(6, 2)
                for c in range(16):
                    tr(pv[:, c * 128:(c + 1) * 128], XB_.ap[:, c * 128:(c + 1) * 128], ident.ap,
                       [XB_.r, ident.r], [PB[6].r, PB[7].r])
                cp("act", XN.ap[:, :, t4 * 128:(t4 + 1) * 128], pv.rearrange("p (c t) -> p c t", c=16),
                   [PB[6].r, PB[7].r], [XN.r])
            dma("pool", XnT_d[:, :, j * 512:(j + 1) * 512].rearrange("c p t -> p c t"), XN.ap, [XN.r], [])

    def phase4a():
        w_q_bf = A.alloc("w_q_bf", [16, D], BF16)
        keysT = A.alloc("keysT", [16, 128], BF16)
        m1 = A.mark()
        stg = [A.alloc(f"stg{i}", [D], F32) for i in range(2)]
        for c in range(16):
            s = stg[c % 2]
            dma("sp", s.ap, w_q[c * 128:(c + 1) * 128, :], [], [s.r])
            wcast(c, w_q_bf.ap[:, c, :], s.ap, v128.ap[:, V_FFNG + c:V_FFNG + c + 1], [s.r, v128.r], [w_q_bf.r])
        kb16 = A.alloc("kb16", [8, 128], BF16)
        for jj, ksrc in enumerate((keys1, keys2)):
            s = stg[jj]
            sv = s.ap[:, 0:1024].rearrange("p (h d) -> p h d", h=8)
            dma("sp", sv, ksrc.rearrange("h n d -> n h d"), [], [s.r])
            cp("dve", kb16.ap, sv, [s.r], [kb16.r])
            pv = ps_bf(0, 1)
            for h in range(8):
                tr(pv[:, h * 128:(h + 1) * 128], kb16.ap[:, h, :], ident.ap, [kb16.r, ident.r], [PB[0].r])
            cp("act", keysT.ap.rearrange("p (h j) n -> p h j n", j=2)[:, :, jj, :],
               pv.rearrange("p (h n) -> p h n", h=8), [PB[0].r], [keysT.r])
        P.barrier()
        A.release(m1)
        xg = [A.alloc(f"xg{i}", [16, 512], BF16) for i in range(2)]
        qT = [A.alloc(f"qT{i}", [16, 512], BF16) for i in range(2)]
        sc = [A.alloc(f"sc{i}", [2048], F32) for i in range(2)]
        t_cnt = 0
        bk = 0
        for j in range(NGO):
            XG, QT = xg[j % 2], qT[j % 2]
            dma("sp", XG.ap, XnT_d[:, :, j * 512:(j + 1) * 512].rearrange("c p t -> p c t"), [], [XG.r])
            for m in range(16):
                pb = PB[bk % 4]
                bk += 1
                for c in range(16):
                    mm(pb.ap, w_q_bf.ap[:, c, m * 128:(m + 1) * 128], XG.ap[:, c, :], c == 0, c == 15,
                       [w_q_bf.r, XG.r], [pb.r])
                cp("act" if m % 2 == 0 else "dve", QT.ap[:, m, :], pb.ap, [pb.r], [QT.r])
            for t4 in range(4):
                SCB = sc[t_cnt % 2]
                t_cnt += 1
                for n4 in range(4):
                    pb = PB[4 + n4]
                    for mm_ in range(4):
                        m = n4 * 4 + mm_
                        mm(pb.ap[:, mm_ * 128:(mm_ + 1) * 128], QT.ap[:, m, t4 * 128:(t4 + 1) * 128], keysT.ap[:, m, :],
                           True, True, [QT.r, keysT.r], [pb.r])
                    cp("act" if n4 % 2 == 0 else "dve", SCB.ap[:, n4 * 512:(n4 + 1) * 512], pb.ap, [pb.r], [SCB.r])
                r0 = j * 512 + t4 * 128
                dma("pool", S_d[r0:r0 + 128, :], SCB.ap, [SCB.r], [])

    def phase4b():
        TH = 32
        sc = [A.alloc(f"sc{i}", [16, 128], F32) for i in range(2)]
        works = [A.alloc(f"wk{m}", [128], F32) for m in range(16)]
        works2 = [A.alloc(f"wkc{h}", [256], F32) for h in range(8)]
        topR = [Res(f"top{m}") for m in range(16)]
        idxR = [Res(f"idx{m}") for m in range(16)]
        ctopR = [Res(f"ctop{h}") for h in range(8)]
        top = A.alloc("top", [16, 16], F32)
        idx = A.alloc("idx", [16, 16], U32)
        idxf = A.alloc("idxf", [16, 16], BF16)
        cand = A.alloc("cand", [8, 256], F32)
        ctop = A.alloc("ctop", [8, 16], F32)
        negm = A.alloc("negm", [8], F32)
        ex = A.alloc("ex", [8, 256], F32)
        msk = A.alloc("msk", [8, 256], F32)
        zs = A.alloc("zs", [8], F32)
        Cb = A.alloc("Cb", [16, 128], BF16)
        idxT2 = [A.alloc(f"idxT{i}", [2, 128], F32) for i in range(2)]
        CT2 = [A.alloc(f"CT{i}", [16, 128], BF16) for i in range(2)]
        E1 = [A.alloc(f"E1_{i}", [TH, 128], BF16) for i in range(2)]
        E2 = [A.alloc(f"E2_{i}", [TH, 128], BF16) for i in range(2)]
        BD = [A.alloc(f"BD_{i}", [TH, 128], BF16) for i in range(2)]
        CE = [A.alloc(f"CE_{i}", [4, 128], BF16) for i in range(2)]
        Gs = [A.alloc(f"Gs{i}", [NI1, 128], BF16) for i in range(2)]
        ntile = NOWN // 128
        mrr = [((m % 2) * 8 + m // 2) for m in range(16)]
        st4 = {"e": 0, "g": 0}

        def front(ti):
            S = sc[ti % 2]
            idxT, CT = idxT2[ti % 2], CT2[ti % 2]
            dma("sp", S.ap, S_d[ti * 128:(ti + 1) * 128, :].rearrange("p (m n) -> p m n", m=16), [], [S.r])
            for m in range(16):
                rr = mrr[m]
                P.add("dve", lambda e, m=m, rr=rr, S=S: e.max(out=top.ap[:, rr, 0:8], in_=S.ap[:, m, :]), [S.r], [topR[rr]])
            for m in range(16):
                rr = mrr[m]
                P.add("dve", lambda e, m=m, rr=rr, S=S: e.match_replace(
                    out=works[m].ap, in_to_replace=top.ap[:, rr, 0:8], in_values=S.ap[:, m, :], imm_value=-1e30),
                    [S.r, topR[rr]], [works[m].r])
            for m in range(16):
                rr = mrr[m]
                P.add("dve", lambda e, m=m, rr=rr: e.max(out=top.ap[:, rr, 8:16], in_=works[m].ap), [works[m].r], [topR[rr]])
            for m in range(16):
                rr = mrr[m]
                P.add("dve", lambda e, m=m, rr=rr, S=S: e.max_index(out=idx.ap[:, rr, 0:8], in_max=top.ap[:, rr, 0:8],
                                                                    in_values=S.ap[:, m, :]), [topR[rr], S.r], [idxR[rr]])
            for m in range(16):
                rr = mrr[m]
                P.add("dve", lambda e, m=m, rr=rr, S=S: e.max_index(out=idx.ap[:, rr, 8:16], in_max=top.ap[:, rr, 8:16],
                                                                    in_values=S.ap[:, m, :]), [topR[rr], S.r], [idxR[rr]])
            cp("dve", idxf.ap, idx.ap, idxR, [idxf.r])
            cv = cand.ap.rearrange("p h (a b) -> p h a b", a=16)
            tt("dve", cv, top.ap[:, 0:8, :].unsqueeze(3).to_broadcast([128, 8, 16, 16]),
               top.ap[:, 8:16, :].unsqueeze(2).to_broadcast([128, 8, 16, 16]), ALU.add, topR, [cand.r])
            for h in range(8):
                P.add("dve", lambda e, h=h: e.max(out=ctop.ap[:, h, 0:8], in_=cand.ap[:, h, :]), [cand.r], [ctopR[h]])
            for h in range(8):
                P.add("dve", lambda e, h=h: e.match_replace(out=works2[h].ap, in_to_replace=ctop.ap[:, h, 0:8],
                                                            in_values=cand.ap[:, h, :], imm_value=-1e30),
                      [cand.r, ctopR[h]], [works2[h].r])
            for h in range(8):
                P.add("dve", lambda e, h=h: e.max(out=ctop.ap[:, h, 8:16], in_=works2[h].ap), [works2[h].r], [ctopR[h]])
            ts("dve", negm.ap, ctop.ap[:, :, 0], -1.0, None, ALU.mult, None, ctopR, [negm.r])
            for h in range(8):
                act(ex.ap[:, h, :], cand.ap[:, h, :], AF.Exp, [cand.r, negm.r], [ex.r], bias=negm.ap[:, h:h + 1])
            tt("dve", msk.ap, cand.ap, ctop.ap[:, :, 15:16].to_broadcast([128, 8, 256]), ALU.is_ge,
               [cand.r] + ctopR, [msk.r])
            tt("dve", ex.ap, ex.ap, msk.ap, ALU.mult, [ex.r, msk.r], [ex.r])
            P.add("dve", lambda e: e.reduce_sum(out=zs.ap, in_=ex.ap, axis=AX.X), [ex.r], [zs.r])
            recip(zs.ap, zs.ap, [zs.r], [zs.r])
            tt("dve", Cb.ap.rearrange("p a (h b) -> p h a b", h=8), ex.ap.rearrange("p h (a b) -> p h a b", a=16),
               zs.ap.unsqueeze(2).unsqueeze(3).to_broadcast([128, 8, 16, 16]), ALU.mult, [ex.r, zs.r], [Cb.r])
            pv = ps_bf(0, 1)
            for jj in range(2):
                tr(pv[:, jj * 128:(jj + 1) * 128], idxf.ap[:, jj * 8:(jj + 1) * 8, :].rearrange("p h k -> p (h k)"),
                   ident.ap, [idxf.r, ident.r], [PB[0].r])
            cp("act", idxT.ap, pv[:, 0:256].rearrange("p (j t) -> p j t", j=2), [PB[0].r], [idxT.r])
            pv2 = ps_bf(1, 2)
            for k1 in range(16):
                tr(pv2[:, k1 * 128:(k1 + 1) * 128], Cb.ap[:, k1, :], ident.ap, [Cb.r, ident.r], [PB[1].r, PB[2].r])
            cp("act", CT.ap, pv2.rearrange("p (a t) -> p a t", a=16), [PB[1].r, PB[2].r], [CT.r])

        def back(ti):
            idxT, CT = idxT2[ti % 2], CT2[ti % 2]
            GS = Gs[ti % 2]
            NQ = TH // 4
            chunks = {}

            def egen(hh):
                tlo = hh * TH
                e1, e2, bd = E1[st4["e"] % 2], E2[st4["e"] % 2], BD[st4["e"] % 2]
                st4["e"] += 1
                tt("dve", e1.ap, iota_f.ap[:, None, :].to_broadcast([128, TH, 128]),
                   idxT.ap[:, 0, tlo:tlo + TH].unsqueeze(2).to_broadcast([128, TH, 128]), ALU.is_equal,
                   [iota_f.r, idxT.r], [e1.r])
                tt("dve", e2.ap, iota_f.ap[:, None, :].to_broadcast([128, TH, 128]),
                   idxT.ap[:, 1, tlo:tlo + TH].unsqueeze(2).to_broadcast([128, TH, 128]), ALU.is_equal,
                   [iota_f.r, idxT.r], [e2.r])
                tt("dve", bd.ap.rearrange("p t (h k) -> p t h k", h=8),
                   CT.ap[:, :, tlo:tlo + TH].rearrange("p k t -> p t k").unsqueeze(2).to_broadcast([128, TH, 8, 16]),
                   bmask.ap[:, None, :].unsqueeze(3).to_broadcast([128, TH, 8, 16]), ALU.mult,
                   [CT.r, bmask.r], [bd.r])
                chunks[hh] = (e1, e2, bd)

            items = [(hh, q4) for hh in range(128 // TH) for q4 in range(NQ)]
            slot = {}

            def stA(k):
                hh, q4 = items[k]
                if q4 == 0:
                    egen(hh)
                e1, e2, bd = chunks[hh]
                g = st4["g"]
                st4["g"] += 1
                pa, pg, ce = PB[3 + (g % 2)], PB[5 + (g % 2)], CE[g % 2]
                slot[k] = (pa, pg, ce)
                for i4 in range(4):
                    tl = q4 * 4 + i4
                    mm(pa.ap[:, i4 * 128:(i4 + 1) * 128], bd.ap[:, tl, :], e2.ap[:, tl, :], True, True,
                       [bd.r, e2.r], [pa.r])
                cp("act", ce.ap, pa.ap.rearrange("p (a n) -> p a n", a=4), [pa.r], [ce.r])

            def stB(k):
                hh, q4 = items[k]
                e1, e2, bd = chunks[hh]
                pa, pg, ce = slot[k]
                for i4 in range(4):
                    tl = q4 * 4 + i4
                    mm(pg.ap[:, i4 * 128:(i4 + 1) * 128], ce.ap[:, i4, :], e1.ap[:, tl, :], True, True,
                       [ce.r, e1.r], [pg.r])
                t0 = hh * TH + q4 * 4
                cp("act", GS.ap[:, :, t0:t0 + 4].rearrange("p i t -> p t i"),
                   pg.ap.rearrange("p (a n) -> p a n", a=4), [pg.r], [GS.r])

            stA(0)
            for k in range(len(items)):
                if k + 1 < len(items):
                    stA(k + 1)
                stB(k)
            gflat = GS.ap.rearrange("p i t -> p (i t)")
            NSPL = 4
            wd = NI1 * 128 // NSPL
            for sp_ in range(NSPL):
                dma("pool", G_d[ti][:, sp_ * wd:(sp_ + 1) * wd], gflat[:, sp_ * wd:(sp_ + 1) * wd], [GS.r], [])

        front(0)
        for ti in range(ntile):
            if ti + 1 < ntile:
                front(ti + 1)
            back(ti)

    def phase5():
        EB = 4
        xg = [A.alloc(f"xg{i}", [16, 512], BF16) for i in range(2)]
        acc = [A.alloc(f"acc{i}", [4, D], F32) for i in range(2)]
        uT = [A.alloc(f"uT{i}", [16, EB * 128], BF16) for i in range(2)]
        vb = [A.alloc(f"vbk{i}", [EB, D], BF16) for i in range(3)]
        gb = [A.alloc(f"gb{i}", [EB, 512], BF16) for i in range(2)]
        gl = [A.alloc(f"gl{i}", [512], BF16) for i in range(2)]
        W = [A.alloc(f"W{i}", [EB, 512], BF16) for i in range(2)]
        outs = []
        NEB = NI1 // EB
        blocks = [(j, eb) for j in range(NGO) for eb in range(NEB)]
        st5 = {"g": 0, "o": 0}

        def stageA(k):
            j, eb = blocks[k]
            XG, AC = xg[j % 2], acc[j % 2]
            if eb == 0:
                dma("sp", XG.ap, XnT_d[:, :, j * 512:(j + 1) * 512].rearrange("c p t -> p c t"), [], [XG.r])
                dma("sp", AC.ap, X1_d[j * 512:(j + 1) * 512, :].rearrange("(a p) d -> p a d", p=128), [], [AC.r])
            UT, VB, GB, WB = uT[k % 2], vb[k % 3], gb[k % 2], W[k % 2]
            e0 = eb * EB * 128
            dma("sp", UT.ap.rearrange("p c e -> p (c e)"), uT_d[eb], [], [UT.r])
            dma("sp", VB.ap, vb_d[e0:e0 + EB * 128, :].rearrange("(a p) d -> p a d", p=128), [], [VB.r])
            for t4 in range(4):
                dma("sp", GB.ap[:, :, t4 * 128:(t4 + 1) * 128],
                    G_d[j * 4 + t4].rearrange("p (i t) -> p i t", t=128)[:, eb * EB:(eb + 1) * EB, :], [], [GB.r])
            for et in range(EB):
                pa = PB[st5["g"] % 2]
                GL = gl[st5["g"] % 2]
                st5["g"] += 1
                for c in range(16):
                    mm(pa.ap, UT.ap[:, c, et * 128:(et + 1) * 128], XG.ap[:, c, :], c == 0, c == 15,
                       [UT.r, XG.r], [pa.r])
                act(GL.ap, pa.ap, AF.Gelu, [pa.r], [GL.r])
                tt("dve", WB.ap[:, et, :], GL.ap, GB.ap[:, et, :], ALU.mult, [GL.r, GB.r], [WB.r])

        def stageB(k):
            j, eb = blocks[k]
            AC = acc[j % 2]
            VB, WB = vb[k % 3], W[k % 2]
            for t4 in range(4):
                for hf in range(2):
                    oc = st5["o"]
                    st5["o"] += 1
                    po = [PB[2 + (oc % 3) * 2], PB[3 + (oc % 3) * 2]]
                    for et in range(EB):
                        for n2 in range(2):
                            col = hf * 1024 + n2 * 512
                            mm(po[n2].ap, WB.ap[:, et, t4 * 128:(t4 + 1) * 128], VB.ap[:, et, col:col + 512],
                               et == 0, et == EB - 1, [WB.r, VB.r], [po[n2].r])
                    for n2 in range(2):
                        col = hf * 1024 + n2 * 512
                        tt("dve", AC.ap[:, t4, col:col + 512], po[n2].ap, AC.ap[:, t4, col:col + 512], ALU.add,
                           [po[n2].r, AC.r], [AC.r])
            if eb == NEB - 1:
                outs.append(dma("pool", out_d[j * 512:(j + 1) * 512, :].rearrange("(a p) d -> p a d", p=128), AC.ap,
                                [AC.r], []))

        stageA(0)
        for k in range(len(blocks)):
            if k + 1 < len(blocks):
                stageA(k + 1)
            stageB(k)
        return outs

    stop_after = [k for k in debug if isinstance(k, str) and k.startswith("stop:")]
    stop = stop_after[0][5:] if stop_after else None
    finals = []

    def run_phase(fn):
        m = A.mark()
        r = fn()
        P.barrier()
        A.release(m)
        return r

    seq = [("tables", phase_tables), ("p1", phase1), ("p23", phase23), ("p4a", phase4a), ("p4b", phase4b),
           ("p5", phase5)]
    skip = set(k[5:] for k in debug if isinstance(k, str) and k.startswith("skip:"))
    last = None
    for name, fn in seq:
        if name in skip:
            continue
        last = run_phase(fn)
        if stop == name:
            break
    if last:
        finals = [("pool", o) for o in last]
    else:
        dummy = A.alloc("dummy", [16], F32)
        P.add("dve", lambda e: e.memset(dummy.ap, 0.0), [], [dummy.r])
        o = dma("pool", out_d[0:128, 0:16], dummy.ap, [dummy.r], [])
        finals = [("pool", o)]
    P.emit(st, final_wait_ops=finals)
    st.close()
    return nc


def _cols(v, n):
    return np.ascontiguousarray(np.asarray(v, np.float32).reshape(n, 128).T)


def make_core_inputs(inp, b, tok0, NOWN, NPRE, has_prefix, NE=16384):
    f = np.float32
    x = inp["x"]
    L = 0
    v128 = np.zeros((128, NV128), f)
    v128[:, V_MIXG:V_MIXG + 16] = _cols(inp["mix_norm_g"][L], 16)
    v128[:, V_FFNG:V_FFNG + 16] = _cols(inp["ffn_norm_g"][L], 16)
    v128[:, V_QNG:V_QNG + 4] = _cols(inp["q_norm_g"][L], 4)
    v128[:, V_KVNG:V_KVNG + 2] = _cols(inp["kv_norm_g"][L], 2)
    v128[:, V_AOG:V_AOG + 8] = _cols(inp["attn_out_norm_g"][L], 8)
    v128[:, V_ROG:V_ROG + 8] = _cols(inp["rec_out_norm_g"][L], 8)
    v128[:, V_QHG] = inp["q_head_norm_g"][L][:128]
    v128[:, V_KHG] = inp["k_head_norm_g"][L][:128]
    for j in range(4):
        v128[:, V_CW + j * 8:V_CW + j * 8 + 8] = _cols(inp["conv_w"][L][j], 8)
    v128[:, V_CB:V_CB + 8] = _cols(inp["conv_b"][L], 8)
    v128[:, V_BRG:V_BRG + 8] = _cols(inp["b_rgate"][L], 8)
    v128[:, V_BIG:V_BIG + 8] = _cols(inp["b_igate"][L], 8)
    v128[:, V_LAM:V_LAM + 8] = _cols(inp["lru_lambda"][L], 8)
    v64 = np.zeros((64, NV64), f)
    qg = np.asarray(inp["q_head_norm_g"][L], f)[128:192]
    kg = np.asarray(inp["k_head_norm_g"][L], f)[128:192]
    v64[:, W_QHG] = qg
    v64[:, W_QHGS] = np.concatenate([qg[32:], qg[:32]])
    v64[:, W_KHG] = kg
    v64[:, W_KHGS] = np.concatenate([kg[32:], kg[:32]])
    invf = (np.float32(10000.0) ** (-np.arange(32, dtype=np.float32) / np.float32(32))).astype(f)
    v64[:, W_INVF] = np.concatenate([invf, invf])
    v64[:, W_SIGN] = np.concatenate([-np.ones(32, f), np.ones(32, f)])
    flags = np.zeros((128, 2), f)
    flags[:, 0] = 1.0 if has_prefix else 0.0
    flags[:, 1] = 0.0 if has_prefix else NEG
    pre0 = tok0 - NPRE if has_prefix else 0
    pos = np.concatenate([inp["positions"][b, pre0:pre0 + NPRE], inp["positions"][b, tok0:tok0 + NOWN]])
    return {
        "x_own": np.ascontiguousarray(x[b, tok0:tok0 + NOWN]),
        "x_pre": np.ascontiguousarray(x[b, pre0:pre0 + NPRE]),
        "pos_all": np.ascontiguousarray(pos.astype(np.int32).reshape(1, -1)),
        "flags": flags,
        "vec128": v128,
        "vec64": v64,
        "ffng_row": np.ascontiguousarray(np.asarray(inp["ffn_norm_g"][L], f).reshape(1, D)),
        "w_in": np.ascontiguousarray(inp["w_in"][L]),
        "w_uq": np.ascontiguousarray(inp["w_uq"][L]),
        "w_ukv": np.ascontiguousarray(inp["w_ukv"][L]),
        "w_out": np.ascontiguousarray(inp["w_out"][L]),
        "w_q": np.ascontiguousarray(inp["peer_w_q"][L]),
        "w_rg": np.ascontiguousarray(inp["w_rgate"][L]),
        "w_ig": np.ascontiguousarray(inp["w_igate"][L]),
        "keys1": np.ascontiguousarray(inp["peer_keys_1"][L]),
        "keys2": np.ascontiguousarray(inp["peer_keys_2"][L]),
        "u_tab": np.ascontiguousarray(inp["peer_u"][L]),
        "v_tab": np.ascontiguousarray(inp["peer_v"][L]),
    }


def kernel(**inputs):
    inp = {k: np.asarray(v) for k, v in inputs.items()}
    B, S, _ = inp["x"].shape
    NOWN = S // 2
    NPRE = S // 2
    nc = build(NOWN, NPRE)
    in_maps = []
    for c in range(2 * B):
        b, half = c // 2, c % 2
        in_maps.append(make_core_inputs(inp, b, half * NOWN, NOWN, NPRE, half == 1))
    res = run_bass_kernel_spmd(nc, in_maps, core_ids=list(range(2 * B)))
    out = np.empty((B, S, D), np.float32)
    for c in range(2 * B):
        b, half = c // 2, c % 2
        out[b, half * NOWN:(half + 1) * NOWN] = res.results[c]["out"]
    return out
```

```python
import math
from contextlib import ExitStack
import numpy as np
import concourse.bass as bass
import concourse.mybir as mybir
from concourse.bass_utils import run_bass_kernel_spmd

F32 = mybir.dt.float32
BF16 = mybir.dt.bfloat16
U32 = mybir.dt.uint32
I32 = mybir.dt.int32
U8 = mybir.dt.uint8
AF = mybir.ActivationFunctionType
ALU = mybir.AluOpType
AX = mybir.AxisListType
DSZ = {F32: 4, BF16: 2, U32: 4, I32: 4, U8: 1}

D = 2048
EPS = 1e-6
NEG = -30000.0
EPOCH = 30000
N_DMA_SEMS = 12


class Res:
    __slots__ = ("name", "last_w", "readers", "excl")

    def __init__(self, name="", excl=False):
        self.name = name
        self.last_w = None
        self.readers = []
        self.excl = excl


class Op:
    __slots__ = ("eng", "fn", "deps", "dma", "needs_inc", "sem_i", "val")

    def __init__(self, eng, fn, dma):
        self.eng = eng
        self.fn = fn
        self.dma = dma
        self.deps = []
        self.needs_inc = False
        self.sem_i = None
        self.val = None


class Prog:
    ENGS = ("pe", "act", "dve", "pool", "sp")

    def __init__(self, nc):
        self.nc = nc
        self.ops = []
        self.last_dma_on_sem = {}
        self.dma_rr = {e: 0 for e in self.ENGS}
        self.last_op = {e: None for e in self.ENGS}
        self.pending_barrier = {e: None for e in self.ENGS}

    def barrier(self):
        deps = [o for o in self.last_op.values() if o is not None]
        deps += list(self.last_dma_on_sem.values())
        for e in self.ENGS:
            self.pending_barrier[e] = deps

    def add(self, eng, fn, reads=(), writes=(), dma=False):
        op = Op(eng, fn, dma)
        deps = set()
        ex = [r for r in reads if r.excl]
        if ex:
            reads = [r for r in reads if not r.excl]
            writes = list(writes) + ex
        for r in reads:
            if r.last_w is not None:
                deps.add(r.last_w)
        for r in writes:
            if r.last_w is not None:
                deps.add(r.last_w)
            for o in r.readers:
                deps.add(o)
        for r in reads:
            r.readers.append(op)
        for r in writes:
            r.last_w = op
            r.readers = []
        deps.discard(op)
        pb = self.pending_barrier[eng]
        if pb is not None:
            deps.update(pb)
            self.pending_barrier[eng] = None
        if dma:
            k = (eng, self.dma_rr[eng] % N_DMA_SEMS)
            self.dma_rr[eng] += 1
            prev = self.last_dma_on_sem.get(k)
            if prev is not None:
                deps.add(prev)
            self.last_dma_on_sem[k] = op
            op.sem_i = k
        for d in deps:
            if d.eng == "pe" and eng == "pe" and not d.dma and not dma and pb is None:
                continue
            op.deps.append(d)
            d.needs_inc = True
        self.ops.append(op)
        self.last_op[eng] = op
        return op

    def emit(self, stack, final_wait_ops=()):
        nc = self.nc
        cnt = {e: 0 for e in self.ENGS}
        dcnt = {}
        n_ep = {e: 1 for e in self.ENGS}
        for (_, op) in final_wait_ops:
            op.needs_inc = True
        for op in self.ops:
            if op.dma:
                dcnt[op.sem_i] = dcnt.get(op.sem_i, 0) + 16
                op.val = dcnt[op.sem_i]
            elif op.needs_inc:
                c = cnt[op.eng]
                cnt[op.eng] = c + 1
                op.sem_i = (op.eng, "c", c // EPOCH)
                op.val = c % EPOCH + 1
                n_ep[op.eng] = c // EPOCH + 1
        sems = {}
        for e in self.ENGS:
            for k in range(n_ep[e]):
                sems[(e, "c", k)] = stack.enter_context(nc.semaphore(f"c_{e}_{k}"))
        for k in dcnt:
            sems[k] = stack.enter_context(nc.semaphore(f"d_{k[0]}_{k[1]}"))
        per_eng = {e: [] for e in self.ENGS}
        for op in self.ops:
            per_eng[op.eng].append(op)
        final = {e: [] for e in self.ENGS}
        for (e, op) in final_wait_ops:
            final[e].append(op)

        def run(eng_name, eng):
            known = {}
            for op in per_eng[eng_name]:
                need = {}
                for d in op.deps:
                    if d.val > need.get(d.sem_i, 0):
                        need[d.sem_i] = d.val
                for k, v in need.items():
                    if known.get(k, 0) >= v:
                        continue
                    eng.wait_ge(sems[k], v)
                    known[k] = v
                inst = op.fn(eng)
                if op.dma:
                    inst.then_inc(sems[op.sem_i], 16)
                elif op.needs_inc:
                    inst.then_inc(sems[op.sem_i], 1)
            for op in final[eng_name]:
                eng.wait_ge(sems[op.sem_i], op.val)

        block = stack.enter_context(nc.Block())

        @block.tensor
        def _(e):
            run("pe", e)

        @block.scalar
        def _(e):
            run("act", e)

        @block.vector
        def _(e):
            run("dve", e)

        @block.gpsimd
        def _(e):
            run("pool", e)

        @block.sync
        def _(e):
            run("sp", e)


class Buf:
    __slots__ = ("ap", "r")

    def __init__(self, ap, name):
        self.ap = ap
        self.r = Res(name)


class Arena:
    def __init__(self, nc, st, nbytes):
        self.t = st.enter_context(nc.sbuf_tensor("arena", [128, nbytes], U8))
        self.n = nbytes
        self.off = 0

    def mark(self):
        return self.off

    def release(self, m):
        self.off = m

    def alloc(self, name, shape, dt):
        n = int(np.prod(shape)) * DSZ[dt]
        n_al = (n + 63) // 64 * 64
        assert self.off + n_al <= self.n, f"arena overflow at {name}: {self.off}+{n_al}>{self.n}"
        ap = self.t[:, self.off:self.off + n].bitcast(dt)
        self.off += n_al
        if len(shape) == 2:
            ap = ap.rearrange("p (a b) -> p a b", a=shape[0])
        elif len(shape) == 3:
            ap = ap.rearrange("p (a b c) -> p a b c", a=shape[0], b=shape[1])
        return Buf(ap, name)


OFF_CQ, OFF_CKV, OFF_KR, OFF_XR, OFF_YG, IN_COLS = 0, 512, 768, 832, 1856, 2880
WIN_W = IN_COLS + 64
WUQ_W = 1536 + 512

V_MIXG, V_FFNG, V_QNG, V_KVNG, V_AOG, V_ROG = 0, 16, 32, 36, 38, 46
V_QHG, V_KHG, V_CW, V_CB, V_BRG, V_BIG, V_LAM = 54, 55, 56, 88, 96, 104, 112
NV128 = 120
W_QHG, W_QHGS, W_KHG, W_KHGS, W_INVF, W_SIGN = 0, 1, 2, 3, 4, 5
NV64 = 6


def build(NOWN, NPRE, NE=16384, debug=()):
    nc = bass.Bass("TRN2", target_bir_lowering=False)
    NT = NOWN + NPRE
    NGP, NGO = NPRE // 512, NOWN // 512
    NG = NGP + NGO
    NI1 = NE // 128

    def din(name, shape, dt=F32):
        return nc.dram_tensor(name, list(shape), dt, kind="ExternalInput").ap()

    def dscr(name, shape, dt):
        kind = "ExternalOutput" if name in debug else "Internal"
        return nc.dram_tensor(name, list(shape), dt, kind=kind).ap()

    x_own = din("x_own", [NOWN, D])
    x_pre = din("x_pre", [NPRE, D])
    pos_all = din("pos_all", [1, NT], I32)
    flags = din("flags", [128, 2])
    vec128 = din("vec128", [128, NV128])
    vec64 = din("vec64", [64, NV64])
    ffng_row = din("ffng_row", [1, D])
    w_in = din("w_in", [D, IN_COLS])
    w_uq = din("w_uq", [512, 1536])
    w_ukv = din("w_ukv", [256, 2048])
    w_out = din("w_out", [D, D])
    w_q = din("w_q", [D, D])
    w_rg = din("w_rg", [8, 128, 128])
    w_ig = din("w_ig", [8, 128, 128])
    keys1 = din("keys1", [8, 128, 128])
    keys2 = din("keys2", [8, 128, 128])
    u_tab = din("u_tab", [NE, D])
    v_tab = din("v_tab", [NE, D])
    out_d = nc.dram_tensor("out", [NOWN, D], F32, kind="ExternalOutput").ap()

    Kn_d = dscr("Kn_d", [8, 128, NT], BF16)
    Kr_d = dscr("Kr_d", [8, 64, NT], BF16)
    V_d = dscr("V_d", [NT, 1024], BF16)
    Qn_d = dscr("Qn_d", [8, 128, NOWN], BF16)
    Qr_d = dscr("Qr_d", [8, 64, NOWN], BF16)
    Rec_d = dscr("Rec_d", [8, 128, NOWN], BF16)
    X1_d = dscr("X1_d", [NOWN, D], F32)
    XnT_d = dscr("XnT_d", [16, 128, NOWN], BF16)
    S_d = dscr("S_d", [NOWN, 2048], F32)
    G_d = dscr("G_d", [NOWN // 128, 128, NI1 * 128], BF16)
    uT_d = dscr("uT_d", [NE // 512, 128, 16 * 512], BF16)
    vb_d = dscr("vb_d", [NE, D], BF16)

    st = ExitStack()
    P = Prog(nc)
    A = Arena(nc, st, 206 * 1024)
    psum_all = st.enter_context(nc.psum_tensor("psum_all", [128, 4096], F32))
    PB = [Buf(psum_all[:, b * 512:(b + 1) * 512], f"psb{b}") for b in range(8)]
    for _b in PB:
        _b.r.excl = True

    def ps_bf(b0, nb):
        return psum_all[:, b0 * 512:(b0 + nb) * 512].bitcast(BF16)

    def dma(q, out, in_, reads, writes):
        return P.add(q, lambda e: e.dma_start(out=out, in_=in_), reads, writes, dma=True)

    def mm(out, lhsT, rhs, start, stop, reads, writes):
        return P.add("pe", lambda e: e.matmul(out, lhsT, rhs, start=start, stop=stop), reads, writes)

    def tr(out, in_, ident, reads, writes):
        return P.add("pe", lambda e: e.transpose(out, in_, ident), reads, writes)

    def act(out, in_, func, reads, writes, bias=None, scale=None, accum=None):
        kw = {}
        if bias is not None:
            kw["bias"] = bias
        if scale is not None:
            kw["scale"] = scale
        if accum is not None:
            kw["accum_out"] = accum
        return P.add("act", lambda e: e.activation(out=out, in_=in_, func=func, **kw), reads, writes)

    def tt(eng, out, in0, in1, op, reads, writes):
        return P.add(eng, lambda e: e.tensor_tensor(out=out, in0=in0, in1=in1, op=op), reads, writes)

    def ts(eng, out, in0, s1, s2, op0, op1, reads, writes):
        if s2 is None:
            return P.add(eng, lambda e: e.tensor_scalar(out=out, in0=in0, scalar1=s1, scalar2=None, op0=op0),
                         reads, writes)
        return P.add(eng, lambda e: e.tensor_scalar(out=out, in0=in0, scalar1=s1, scalar2=s2, op0=op0, op1=op1),
                     reads, writes)

    def stt(eng, out, in0, scalar, in1, op0, op1, reads, writes):
        return P.add(eng, lambda e: e.scalar_tensor_tensor(out=out, in0=in0, scalar=scalar, in1=in1, op0=op0, op1=op1),
                     reads, writes)

    def cp(eng, out, in_, reads, writes):
        if eng == "act":
            return act(out, in_, AF.Copy, reads, writes)
        return P.add(eng, lambda e: e.tensor_copy(out=out, in_=in_), reads, writes)

    def wcast(i, out, in_, g, reads, writes):
        if i % 2 == 0:
            return ts("dve", out, in_, g, None, ALU.mult, None, reads, writes)
        return P.add("act", lambda e: e.mul(out=out, in_=in_, mul=g), reads, writes)

    def recip(out, in_, reads, writes):
        return P.add("dve", lambda e: e.reciprocal(out=out, in_=in_), reads, writes)

    def rstd_from_ps(dst, src_ap, src_res, n):
        ts("dve", dst.ap, src_ap, 1.0 / n, EPS, ALU.mult, ALU.add, src_res, [dst.r])
        act(dst.ap, dst.ap, AF.Ln, [dst.r], [dst.r])
        act(dst.ap, dst.ap, AF.Exp, [dst.r], [dst.r], scale=-0.5)

    def dbgdump(name, ap, res, shape, dt=F32):
        if name in debug:
            d_ = nc.dram_tensor(name, list(shape), dt, kind="ExternalOutput").ap()
            dma("pool", d_, ap, [res], [])

    ident = A.alloc("ident", [128], BF16)
    ones = A.alloc("ones", [128], BF16)
    iota_f = A.alloc("iota_f", [128], F32)
    tri = A.alloc("tri", [128], BF16)
    bmask = A.alloc("bmask", [8], BF16)
    v128 = A.alloc("v128", [NV128], F32)
    v64 = A.alloc("v64", [NV64], F32)
    flg = A.alloc("flg", [2], F32)
    cL = A.alloc("cL", [8], F32)
    gsc = A.alloc("gsc", [8], F32)
    tmpc = A.alloc("tmpc", [128], F32)

    dma("sp", v128.ap, vec128, [], [v128.r])
    dma("sp", v64.ap[0:64], vec64, [], [v64.r])
    dma("sp", flg.ap, flags, [], [flg.r])
    P.add("pool", lambda e: e.iota(iota_f.ap, pattern=[[1, 128]], base=0, channel_multiplier=0,
                                   allow_small_or_imprecise_dtypes=True), [], [iota_f.r])
    P.add("pool", lambda e: e.iota(tmpc.ap, pattern=[[1, 128]], base=0, channel_multiplier=-1,
                                   allow_small_or_imprecise_dtypes=True), [], [tmpc.r])
    P.add("dve", lambda e: e.tensor_single_scalar(out=ident.ap, in_=tmpc.ap, scalar=0.0, op=ALU.is_equal),
          [tmpc.r], [ident.r])
    P.add("dve", lambda e: e.tensor_single_scalar(out=tri.ap, in_=tmpc.ap, scalar=0.0, op=ALU.is_ge),
          [tmpc.r], [tri.r])
    P.add("dve", lambda e: e.memset(ones.ap, 1.0), [], [ones.r])
    P.add("pool", lambda e: e.iota(tmpc.ap[:, 0:8], pattern=[[-16, 8]], base=0, channel_multiplier=1,
                                   allow_small_or_imprecise_dtypes=True), [tmpc.r], [tmpc.r])
    P.add("dve", lambda e: e.tensor_single_scalar(out=tmpc.ap[:, 8:16], in_=tmpc.ap[:, 0:8], scalar=0.0, op=ALU.is_ge),
          [tmpc.r], [tmpc.r])
    P.add("dve", lambda e: e.tensor_single_scalar(out=tmpc.ap[:, 16:24], in_=tmpc.ap[:, 0:8], scalar=15.0, op=ALU.is_le),
          [tmpc.r], [tmpc.r])
    tt("dve", bmask.ap, tmpc.ap[:, 8:16], tmpc.ap[:, 16:24], ALU.mult, [tmpc.r], [bmask.r])
    act(cL.ap, v128.ap[:, V_LAM:V_LAM + 8], AF.Exp, [v128.r], [cL.r], scale=-1.0)
    act(cL.ap, cL.ap, AF.Ln, [cL.r], [cL.r], bias=1.0)
    ts("dve", cL.ap, cL.ap, -8.0, None, ALU.mult, None, [cL.r], [cL.r])
    SC = 192.0 ** -0.5
    ts("dve", gsc.ap[:, 0:1], v128.ap[:, V_QHG:V_QHG + 1], SC, None, ALU.mult, None, [v128.r], [gsc.r])
    ts("dve", gsc.ap[0:64, 1:2], v64.ap[0:64, W_QHG:W_QHG + 1], SC, None, ALU.mult, None, [v64.r], [gsc.r])
    ts("dve", gsc.ap[0:64, 2:3], v64.ap[0:64, W_QHGS:W_QHGS + 1], v64.ap[0:64, W_SIGN:W_SIGN + 1], SC,
       ALU.mult, ALU.mult, [v64.r], [gsc.r])
    ts("dve", gsc.ap[0:64, 3:4], v64.ap[0:64, W_KHGS:W_KHGS + 1], v64.ap[0:64, W_SIGN:W_SIGN + 1], None,
       ALU.mult, None, [v64.r], [gsc.r])

    m_const = A.mark()

    def phase_tables():
        grow = A.alloc("grow", [D], F32)
        dma("sp", grow.ap, ffng_row.partition_broadcast(128), [], [grow.r])
        TE = 4
        ust = [A.alloc("ust0", [TE, D], F32)]
        ub = [A.alloc("ub0", [TE, D], BF16)]
        uTs = [A.alloc(f"uTs{i}", [16, TE * 128], BF16) for i in range(2)]
        vst = [A.alloc("vst0", [TE, D], F32)]
        vb = [A.alloc("vb0", [TE, D], BF16)]
        for it in range(NE // (TE * 128)):
            e0 = it * TE * 128
            UTS = uTs[it % 2]
            dma("sp", ust[0].ap, u_tab[e0:e0 + TE * 128, :].rearrange("(a p) d -> p a d", p=128), [], [ust[0].r])
            dma("sp", vst[0].ap, v_tab[e0:e0 + TE * 128, :].rearrange("(a p) d -> p a d", p=128), [], [vst[0].r])
            tt("dve", ub[0].ap, ust[0].ap, grow.ap[:, None, :].to_broadcast([128, TE, D]), ALU.mult,
               [ust[0].r, grow.r], [ub[0].r])
            for et in range(TE):
                b0 = (et % 4) * 2
                pv = ps_bf(b0, 2)
                for c in range(16):
                    tr(pv[:, c * 128:(c + 1) * 128], ub[0].ap[:, et, c * 128:(c + 1) * 128], ident.ap,
                       [ub[0].r, ident.r], [PB[b0].r, PB[b0 + 1].r])
                cp("act", UTS.ap[:, :, et * 128:(et + 1) * 128],
                   pv.rearrange("p (c e) -> p c e", c=16), [PB[b0].r, PB[b0 + 1].r], [UTS.r])
            dma("pool", uT_d[it], UTS.ap.rearrange("p c e -> p (c e)"), [UTS.r], [])
            cp("dve", vb[0].ap, vst[0].ap, [vst[0].r], [vb[0].r])
            dma("pool", vb_d[e0:e0 + TE * 128, :].rearrange("(a p) d -> p a d", p=128), vb[0].ap, [vb[0].r], [])

    def phase1():
        G1 = 256
        T1 = G1 // 128
        w_in_bf = A.alloc("w_in_bf", [16, WIN_W], BF16)
        w_uq_bf = A.alloc("w_uq_bf", [4, WUQ_W], BF16)
        w_ukv_bf = A.alloc("w_ukv_bf", [2, 2048], BF16)
        w_rg_bf = A.alloc("w_rg_bf", [8, 128], BF16)
        w_ig_bf = A.alloc("w_ig_bf", [8, 128], BF16)
        m1 = A.mark()
        stg = [A.alloc(f"stg{i}", [IN_COLS], F32) for i in range(2)]
        for c in range(16):
            s = stg[c % 2]
            g = v128.ap[:, V_MIXG + c:V_MIXG + c + 1]
            dma("sp", s.ap, w_in[c * 128:(c + 1) * 128, :], [], [s.r])
            wcast(c, w_in_bf.ap[:, c, 0:IN_COLS], s.ap, g, [s.r, v128.r], [w_in_bf.r])
            ts("dve", w_in_bf.ap[:, c, IN_COLS:IN_COLS + 32], s.ap[:, OFF_KR + 32:OFF_KR + 64], g, None, ALU.mult, None,
               [s.r, v128.r], [w_in_bf.r])
            ts("dve", w_in_bf.ap[:, c, IN_COLS + 32:IN_COLS + 64], s.ap[:, OFF_KR:OFF_KR + 32], g, None, ALU.mult, None,
               [s.r, v128.r], [w_in_bf.r])
        for c in range(4):
            s = stg[c % 2]
            g = v128.ap[:, V_QNG + c:V_QNG + c + 1]
            dma("sp", s.ap[:, 0:1536], w_uq[c * 128:(c + 1) * 128, :], [], [s.r])
            ts("dve", w_uq_bf.ap[:, c, 0:1536], s.ap[:, 0:1536], g, None, ALU.mult, None, [s.r, v128.r], [w_uq_bf.r])
            sv = s.ap[:, 0:1536].rearrange("p (h k) -> p h k", h=8)
            dv = w_uq_bf.ap[:, c, 1536:2048].rearrange("p (h k) -> p h k", h=8)
            ts("dve", dv[:, :, 0:32], sv[:, :, 160:192], g, None, ALU.mult, None, [s.r, v128.r], [w_uq_bf.r])
            ts("dve", dv[:, :, 32:64], sv[:, :, 128:160], g, None, ALU.mult, None, [s.r, v128.r], [w_uq_bf.r])
        for c in range(2):
            s = stg[c % 2]
            g = v128.ap[:, V_KVNG + c:V_KVNG + c + 1]
            dma("sp", s.ap[:, 0:2048], w_ukv[c * 128:(c + 1) * 128, :], [], [s.r])
            ts("dve", w_ukv_bf.ap[:, c, :], s.ap[:, 0:2048], g, None, ALU.mult, None, [s.r, v128.r], [w_ukv_bf.r])
        for (wsrc, wdst) in ((w_rg, w_rg_bf), (w_ig, w_ig_bf)):
            s = stg[0]
            dma("sp", s.ap[:, 0:1024].rearrange("p (h j) -> p h j", h=8), wsrc.rearrange("h i j -> i h j"), [], [s.r])
            cp("dve", wdst.ap, s.ap[:, 0:1024].rearrange("p (h j) -> p h j", h=8), [s.r], [wdst.r])
        P.barrier()
        A.release(m1)

        def al(name, shape, dt, n=1):
            return [A.alloc(f"{name}{i}", shape, dt) for i in range(n)]
        xt = al("xt", [D], F32, 1)
        ss1 = al("ss1_", [1], F32, 2)
        xb = al("xb", [D], BF16, 1)
        hT = al("hT", [16, G1], BF16, 1)
        cq_f = A.alloc("cq_f", [4, G1], F32)
        ckv_f = A.alloc("ckv_f", [2, G1], F32)
        cqn = A.alloc("cqn", [4, G1], BF16)
        ckvn = A.alloc("ckvn", [2, G1], BF16)
        sqb = al("sqb", [G1], BF16, 3)
        sqr = al("sqr", [G1], BF16, 2)
        rs = al("rs", [G1], F32, 3)
        kr_f = A.alloc("kr_f", [G1], F32)
        krs_f = A.alloc("krs_f", [G1], F32)
        kro = A.alloc("kro", [G1], F32)
        sqkr = A.alloc("sqkr", [G1], BF16)
        posi = A.alloc("posi", [G1], I32)
        ang = A.alloc("ang", [G1], F32)
        ang2 = A.alloc("ang2", [G1], F32)
        tq = A.alloc("tq", [G1], F32)
        CC = A.alloc("CC", [G1], F32)
        SS = A.alloc("SS", [G1], F32)
        tA = al("tA", [G1], F32, 1)
        tB = al("tB", [G1], F32, 1)
        qn_o = al("qn_o", [G1], BF16, 2)
        qr_o = al("qr_o", [G1], BF16, 2)
        kn_o = al("kn_o", [G1], BF16, 2)
        kr_o = al("kr_o", [G1], BF16, 2)
        v_o = al("v_o", [1024], BF16, 1)
        xrf = al("xrf", [G1 + 3], F32, 2)
        halo = A.alloc("halo", [8, 3], F32)
        hstate = A.alloc("hstate", [8], F32)
        c0 = al("c0_", [G1], F32, 2)
        xc = al("xc", [G1], F32, 2)
        xcb = al("xcb", [G1], BF16, 2)
        rg = al("rg", [G1], F32, 2)
        ig = al("ig", [G1], F32, 2)
        av = al("av", [G1], F32, 2)
        a2 = al("a2", [G1], F32, 2)
        hs = al("hs", [G1], F32, 2)
        gy = al("gy", [G1], F32, 2)
        rec_f = A.alloc("rec_f", [8, G1], F32)
        rec_n = al("rec_n", [8, G1], BF16, 1)

        P.add("dve", lambda e: e.memset(halo.ap, 0.0), [], [halo.r])
        P.add("dve", lambda e: e.memset(hstate.ap, 0.0), [], [hstate.r])

        bank_rr = [2]

        def nb():
            b = bank_rr[0]
            bank_rr[0] = 2 + (b - 2 + 1) % 5
            return PB[b]

        cnt = {"t": 0, "sq": 0, "rs": 0, "h": 0, "o": 0}
        NG1 = NT // G1
        NGP1 = NPRE // G1
        import os as _os
        _lim = int(_os.environ.get("LIM", "999"))
        _sub = _os.environ.get("SUB", "")
        for gi in range(NG1):
            if gi >= _lim:
                break
            own = gi >= NGP1
            go = gi - NGP1
            xsrc = x_own if own else x_pre
            t0g = (go if own else gi) * G1
            tg_all = gi * G1
            hTg = hT[0]
            for t4 in range(T1):
                k = cnt["t"]
                cnt["t"] += 1
                X, XB_, S1 = xt[0], xb[0], ss1[k % 2]
                dma("sp", X.ap, xsrc[t0g + t4 * 128:t0g + (t4 + 1) * 128, :], [], [X.r])
                P.add("pool", lambda e, S1=S1: e.memset(S1.ap, 0.0), [], [S1.r])
                act(XB_.ap, X.ap, AF.Square, [X.r, S1.r], [XB_.r, S1.r], accum=S1.ap)
                rstd_from_ps(S1, S1.ap, [S1.r], D)
                ts("dve", XB_.ap, X.ap, S1.ap[:, 0:1], None, ALU.mult, None, [X.r, S1.r], [XB_.r])
                pv = ps_bf(0, 2)
                for c in range(16):
                    tr(pv[:, c * 128:(c + 1) * 128], XB_.ap[:, c * 128:(c + 1) * 128], ident.ap,
                       [XB_.r, ident.r], [PB[0].r, PB[1].r])
                cp("act" if t4 % 2 == 0 else "dve", hTg.ap[:, :, t4 * 128:(t4 + 1) * 128],
                   pv.rearrange("p (c t) -> p c t", c=16), [PB[0].r, PB[1].r], [hTg.r])

            def proj(col0, M):
                pb = nb()
                for c in range(16):
                    mm(pb.ap[0:M, 0:G1], w_in_bf.ap[:, c, col0:col0 + M], hTg.ap[:, c, :], c == 0, c == 15,
                       [w_in_bf.r, hTg.r], [pb.r])
                return pb

            def norm_chunks(col0, nch, dst_f, dst_n, n):
                pss = nb()
                for c in range(nch):
                    pb = proj(col0 + c * 128, 128)
                    sq = sqb[cnt["sq"] % 3]
                    cnt["sq"] += 1
                    cp("dve", dst_f.ap[:, c, :], pb.ap[:, 0:G1], [pb.r], [dst_f.r])
                    act(sq.ap, dst_f.ap[:, c, :], AF.Square, [dst_f.r], [sq.r])
                    mm(pss.ap[:, 0:G1], ones.ap, sq.ap, c == 0, c == nch - 1, [ones.r, sq.r], [pss.r])
                r = rs[cnt["rs"] % 3]
                cnt["rs"] += 1
                rstd_from_ps(r, pss.ap[:, 0:G1], [pss.r], n)
                tt("dve", dst_n.ap, dst_f.ap, r.ap[:, None, :].to_broadcast([128, nch, G1]), ALU.mult,
                   [dst_f.r, r.r], [dst_n.r])

            if _sub == "a":
                continue
            _sb = int(_os.environ.get("SUBB", "99"))
            dma("sp", posi.ap[0:64], pos_all[:, tg_all:tg_all + G1].partition_broadcast(64), [], [posi.r])
            cp("dve", ang.ap[0:64], posi.ap[0:64], [posi.r], [ang.r])
            if _sb >= 1:
                ts("dve", ang.ap[0:64], ang.ap[0:64], v64.ap[0:64, W_INVF:W_INVF + 1], None, ALU.mult, None,
                   [ang.r, v64.r], [ang.r])
                ts("dve", ang2.ap[0:64], ang.ap[0:64], float(np.pi / 2), None, ALU.add, None, [ang.r], [ang2.r])
            for (src, dst) in ((ang, SS), (ang2, CC)):
                if _sb >= 2:
                    ts("dve", tq.ap[0:64], src.ap[0:64], float(1.0 / (2 * np.pi)), None, ALU.mult, None, [src.r], [tq.r])
                    cp("dve", posi.ap[0:64], tq.ap[0:64], [tq.r], [posi.r])
                    cp("dve", tq.ap[0:64], posi.ap[0:64], [posi.r], [tq.r])
                if _sb >= 3:
                    stt("dve", tq.ap[0:64], tq.ap[0:64], float(-2 * np.pi), src.ap[0:64], ALU.mult, ALU.add,
                        [tq.r, src.r], [tq.r])
                if _sb >= 4:
                    zz = tA[0]
                    qq = tB[0]
                    tt("dve", zz.ap[0:64], tq.ap[0:64], tq.ap[0:64], ALU.mult, [tq.r], [zz.r])
                    cs = [-1.0 / 6, 1.0 / 120, -1.0 / 5040, 1.0 / 362880, -1.0 / 39916800, 1.0 / 6227020800,
                          -1.0 / 1307674368000]
                    ts("dve", qq.ap[0:64], zz.ap[0:64], cs[6], None, ALU.mult, None, [zz.r], [qq.r])
                    for kk in range(5, -1, -1):
                        stt("dve", qq.ap[0:64], qq.ap[0:64], cs[kk], zz.ap[0:64], ALU.add, ALU.mult,
                            [qq.r, zz.r], [qq.r])
                    stt("dve", dst.ap[0:64], qq.ap[0:64], 1.0, tq.ap[0:64], ALU.add, ALU.mult, [qq.r, tq.r], [dst.r])
            if _sub == "b":
                continue
            norm_chunks(OFF_CKV, 2, ckv_f, ckvn, 256)
            pb = proj(OFF_KR, 64)
            cp("act", kr_f.ap[0:64], pb.ap[0:64, 0:G1], [pb.r], [kr_f.r])
            pb = proj(IN_COLS, 64)
            cp("act", krs_f.ap[0:64], pb.ap[0:64, 0:G1], [pb.r], [krs_f.r])
            act(sqkr.ap[0:64], kr_f.ap[0:64], AF.Square, [kr_f.r], [sqkr.r])
            stt("dve", tA[0].ap[0:64], kr_f.ap[0:64], v64.ap[0:64, W_KHG:W_KHG + 1], CC.ap[0:64], ALU.mult, ALU.mult,
                [kr_f.r, v64.r, CC.r], [tA[0].r])
            stt("dve", tB[0].ap[0:64], krs_f.ap[0:64], gsc.ap[0:64, 3:4], SS.ap[0:64], ALU.mult, ALU.mult,
                [krs_f.r, gsc.r, SS.r], [tB[0].r])
            tt("dve", kro.ap[0:64], tA[0].ap[0:64], tB[0].ap[0:64], ALU.add, [tA[0].r, tB[0].r], [kro.r])
            for h in range(8):
                KN, KR = kn_o[h % 2], kr_o[h % 2]
                pk = nb()
                for c in range(2):
                    mm(pk.ap[:, 0:G1], w_ukv_bf.ap[:, c, h * 256:h * 256 + 128], ckvn.ap[:, c, :], c == 0, c == 1,
                       [w_ukv_bf.r, ckvn.r], [pk.r])
                sq = sqb[cnt["sq"] % 3]
                cnt["sq"] += 1
                act(sq.ap, pk.ap[:, 0:G1], AF.Square, [pk.r], [sq.r])
                pss = nb()
                mm(pss.ap[:, 0:G1], ones.ap, sq.ap, True, False, [ones.r, sq.r], [pss.r])
                mm(pss.ap[:, 0:G1], ones.ap[0:64, :], sqkr.ap[0:64], False, True, [ones.r, sqkr.r], [pss.r])
                r = rs[cnt["rs"] % 3]
                cnt["rs"] += 1
                rstd_from_ps(r, pss.ap[:, 0:G1], [pss.r], 192)
                stt("dve", KN.ap, pk.ap[:, 0:G1], v128.ap[:, V_KHG:V_KHG + 1], r.ap, ALU.mult, ALU.mult,
                    [pk.r, v128.r, r.r], [KN.r])
                tt("dve", KR.ap[0:64], kro.ap[0:64], r.ap[0:64], ALU.mult, [kro.r, r.r], [KR.r])
                dma("pool", Kn_d[h, :, tg_all:tg_all + G1], KN.ap, [KN.r], [])
                dma("pool", Kr_d[h, :, tg_all:tg_all + G1], KR.ap[0:64], [KR.r], [])
            if _sub == "c":
                continue
            wv = [w_ukv_bf.ap[:, c, :].rearrange("p (h k) -> p h k", h=8) for c in range(2)]
            for t4 in range(T1):
                VO = v_o[0]
                cnt["o"] += 1
                for n2 in range(2):
                    pvv = nb()
                    for c in range(2):
                        mm(pvv.ap.rearrange("p (h k) -> p h k", h=4), ckvn.ap[:, c, t4 * 128:(t4 + 1) * 128],
                           wv[c][:, n2 * 4:(n2 + 1) * 4, 128:256], c == 0, c == 1, [ckvn.r, w_ukv_bf.r], [pvv.r])
                    cp("act", VO.ap[:, n2 * 512:(n2 + 1) * 512], pvv.ap, [pvv.r], [VO.r])
                r0 = tg_all + t4 * 128
                dma("pool", V_d[r0:r0 + 128, :], VO.ap, [VO.r], [])

            if _sub == "d":
                continue
            if own:
                norm_chunks(OFF_CQ, 4, cq_f, cqn, 512)
                _qs = int(_os.environ.get("QS", "99"))
                for h in range(8 if _qs > 0 else 0):
                    QN, QR = qn_o[h % 2], qr_o[h % 2]
                    pqn, pqr, pqs = nb(), nb(), nb()
                    for (pb_, col, M) in ((pqn, h * 192, 128), (pqr, h * 192 + 128, 64), (pqs, 1536 + h * 64, 64)):
                        for c in range(4):
                            mm(pb_.ap[0:M, 0:G1], w_uq_bf.ap[:, c, col:col + M], cqn.ap[:, c, :], c == 0, c == 3,
                               [w_uq_bf.r, cqn.r], [pb_.r])
                    sq = sqb[cnt["sq"] % 3]
                    cnt["sq"] += 1
                    sq2 = sqr[h % 2]
                    act(sq.ap, pqn.ap[:, 0:G1], AF.Square, [pqn.r], [sq.r])
                    act(sq2.ap[0:64], pqr.ap[0:64, 0:G1], AF.Square, [pqr.r], [sq2.r])
                    pss = nb()
                    mm(pss.ap[:, 0:G1], ones.ap, sq.ap, True, False, [ones.r, sq.r], [pss.r])
                    mm(pss.ap[:, 0:G1], ones.ap[0:64, :], sq2.ap[0:64], False, True, [ones.r, sq2.r], [pss.r])
                    r = rs[cnt["rs"] % 3]
                    cnt["rs"] += 1
                    rstd_from_ps(r, pss.ap[:, 0:G1], [pss.r], 192)
                    stt("dve", QN.ap, pqn.ap[:, 0:G1], gsc.ap[:, 0:1], r.ap, ALU.mult, ALU.mult,
                        [pqn.r, gsc.r, r.r], [QN.r])
                    TA, TB = tA[0], tB[0]
                    stt("dve", TA.ap[0:64], pqr.ap[0:64, 0:G1], gsc.ap[0:64, 1:2], r.ap[0:64], ALU.mult, ALU.mult,
                        [pqr.r, gsc.r, r.r], [TA.r])
                    stt("dve", TB.ap[0:64], pqs.ap[0:64, 0:G1], gsc.ap[0:64, 2:3], r.ap[0:64], ALU.mult, ALU.mult,
                        [pqs.r, gsc.r, r.r], [TB.r])
                    tt("dve", TA.ap[0:64], TA.ap[0:64], CC.ap[0:64], ALU.mult, [TA.r, CC.r], [TA.r])
                    tt("dve", TB.ap[0:64], TB.ap[0:64], SS.ap[0:64], ALU.mult, [TB.r, SS.r], [TB.r])
                    tt("dve", QR.ap[0:64], TA.ap[0:64], TB.ap[0:64], ALU.add, [TA.r, TB.r], [QR.r])
                    dma("pool", Qn_d[h, :, t0g:t0g + G1], QN.ap, [QN.r], [])
                    dma("pool", Qr_d[h, :, t0g:t0g + G1], QR.ap[0:64], [QR.r], [])

            if _sub == "e":
                continue
            if own and go == 0:
                ts("dve", hstate.ap, hstate.ap, flg.ap[:, 0:1], None, ALU.mult, None, [hstate.r, flg.r], [hstate.r])
                ts("dve", halo.ap, halo.ap, flg.ap[:, 0:1], None, ALU.mult, None, [halo.r, flg.r], [halo.r])
            pss_rec = PB[7]
            _rs = int(_os.environ.get("RS", "999"))
            for h in range(8):
                k = cnt["h"]
                cnt["h"] += 1
                s2 = k % 2
                XR = xrf[s2]
                if 1 <= _rs:
                    pb = proj(OFF_XR + h * 128, 128)
                if 2 <= _rs:
                    cp("dve", XR.ap[:, 0:3], halo.ap[:, h, :], [halo.r], [XR.r])
                if 3 <= _rs:
                    cp("act", XR.ap[:, 3:G1 + 3], pb.ap[:, 0:G1], [pb.r], [XR.r])
                if 4 <= _rs:
                    cp("dve", halo.ap[:, h, :], XR.ap[:, G1:G1 + 3], [XR.r], [halo.r])
                cw = lambda j: v128.ap[:, V_CW + j * 8 + h:V_CW + j * 8 + h + 1]
                if 5 <= _rs:
                    ts("dve", c0[s2].ap, XR.ap[:, 0:G1], cw(0), v128.ap[:, V_CB + h:V_CB + h + 1], ALU.mult, ALU.add,
                       [XR.r, v128.r], [c0[s2].r])
                if 6 <= _rs:
                    stt("dve", c0[s2].ap, XR.ap[:, 1:G1 + 1], cw(1), c0[s2].ap, ALU.mult, ALU.add,
                        [XR.r, v128.r, c0[s2].r], [c0[s2].r])
                if 7 <= _rs:
                    stt("dve", c0[s2].ap, XR.ap[:, 2:G1 + 2], cw(2), c0[s2].ap, ALU.mult, ALU.add,
                        [XR.r, v128.r, c0[s2].r], [c0[s2].r])
                if 8 <= _rs:
                    stt("dve", xc[s2].ap, XR.ap[:, 3:G1 + 3], cw(3), c0[s2].ap, ALU.mult, ALU.add,
                        [XR.r, v128.r, c0[s2].r], [xc[s2].r])
                if 9 <= _rs:
                    cp("act", xcb[s2].ap, xc[s2].ap, [xc[s2].r], [xcb[s2].r])
                pr, pi = nb(), nb()
                if 10 <= _rs:
                    mm(pr.ap[:, 0:G1], w_rg_bf.ap[:, h, :], xcb[s2].ap, True, True, [w_rg_bf.r, xcb[s2].r], [pr.r])
                if 11 <= _rs:
                    mm(pi.ap[:, 0:G1], w_ig_bf.ap[:, h, :], xcb[s2].ap, True, True, [w_ig_bf.r, xcb[s2].r], [pi.r])
                if 12 <= _rs:
                    act(rg[s2].ap, pr.ap[:, 0:G1], AF.Sigmoid, [pr.r, v128.r], [rg[s2].r],
                        bias=v128.ap[:, V_BRG + h:V_BRG + h + 1])
                if 13 <= _rs:
                    act(ig[s2].ap, pi.ap[:, 0:G1], AF.Sigmoid, [pi.r, v128.r], [ig[s2].r],
                        bias=v128.ap[:, V_BIG + h:V_BIG + h + 1])
                if 14 <= _rs:
                    ts("dve", av[s2].ap, rg[s2].ap, cL.ap[:, h:h + 1], None, ALU.mult, None, [rg[s2].r, cL.r], [av[s2].r])
                if 15 <= _rs:
                    act(av[s2].ap, av[s2].ap, AF.Exp, [av[s2].r], [av[s2].r])
                if 16 <= _rs:
                    tt("dve", a2[s2].ap, av[s2].ap, av[s2].ap, ALU.mult, [av[s2].r], [a2[s2].r])
                if 17 <= _rs:
                    act(a2[s2].ap, a2[s2].ap, AF.Ln, [a2[s2].r], [a2[s2].r], scale=-1.0, bias=1.0)
                    act(a2[s2].ap, a2[s2].ap, AF.Exp, [a2[s2].r], [a2[s2].r], scale=0.5)
                if 18 <= _rs:
                    tt("dve", ig[s2].ap, ig[s2].ap, xc[s2].ap, ALU.mult, [ig[s2].r, xc[s2].r], [ig[s2].r])
                if 19 <= _rs:
                    tt("dve", ig[s2].ap, ig[s2].ap, a2[s2].ap, ALU.mult, [ig[s2].r, a2[s2].r], [ig[s2].r])
                if 20 <= _rs:
                    P.add("dve", lambda e, s2=s2, h=h: e.tensor_tensor_scan(
                        out=hs[s2].ap, data0=av[s2].ap, data1=ig[s2].ap, initial=hstate.ap[:, h:h + 1],
                        op0=ALU.mult, op1=ALU.add), [av[s2].r, ig[s2].r, hstate.r], [hs[s2].r])
                if 21 <= _rs:
                    cp("act", hstate.ap[:, h:h + 1], hs[s2].ap[:, G1 - 1:G1], [hs[s2].r], [hstate.r])
                if gi == NGP1 and h == 0:
                    dbgdump("d_xc", xc[s2].ap, xc[s2].r, [128, G1])
                    dbgdump("d_rg", rg[s2].ap, rg[s2].r, [128, G1])
                    dbgdump("d_b", ig[s2].ap, ig[s2].r, [128, G1])
                    dbgdump("d_a", av[s2].ap, av[s2].r, [128, G1])
                    dbgdump("d_hs", hs[s2].ap, hs[s2].r, [128, G1])
                    dbgdump("d_xr", XR.ap, XR.r, [128, G1 + 3])
                if own:
                    pb = proj(OFF_YG + h * 128, 128)
                    act(gy[s2].ap, pb.ap[:, 0:G1], AF.Gelu, [pb.r], [gy[s2].r])
                    if gi == NGP1 and h == 0:
                        dbgdump("d_gy", gy[s2].ap, gy[s2].r, [128, G1])
                    tt("dve", rec_f.ap[:, h, :], gy[s2].ap, hs[s2].ap, ALU.mult, [gy[s2].r, hs[s2].r], [rec_f.r])
                    sq = sqb[cnt["sq"] % 3]
                    cnt["sq"] += 1
                    act(sq.ap, rec_f.ap[:, h, :], AF.Square, [rec_f.r], [sq.r])
                    mm(pss_rec.ap[:, 0:G1], ones.ap, sq.ap, h == 0, h == 7, [ones.r, sq.r], [pss_rec.r])
            if own:
                r = rs[cnt["rs"] % 3]
                cnt["rs"] += 1
                rstd_from_ps(r, pss_rec.ap[:, 0:G1], [pss_rec.r], 1024)
                RN = rec_n[0]
                tt("dve", RN.ap, rec_f.ap, r.ap[:, None, :].to_broadcast([128, 8, G1]), ALU.mult,
                   [rec_f.r, r.r], [RN.r])
                dma("pool", Rec_d[:, :, t0g:t0g + G1].rearrange("h p t -> p h t"), RN.ap, [RN.r], [])

    def phase23():
        w_out_bf = A.alloc("w_out_bf", [16, D], BF16)
        m1 = A.mark()
        stg = [A.alloc(f"stg{i}", [D], F32) for i in range(2)]
        for c in range(16):
            s = stg[c % 2]
            g = v128.ap[:, V_AOG + c:V_AOG + c + 1]
            dma("sp", s.ap, w_out[c * 128:(c + 1) * 128, :], [], [s.r])
            wcast(c, w_out_bf.ap[:, c, :], s.ap, g, [s.r, v128.r], [w_out_bf.r])
        P.barrier()
        A.release(m1)
        NKB = 4
        NPT = 6
        DEPTH = 4
        SB = [PB[0], PB[1], PB[5], PB[6], PB[7]]
        NS = len(SB)
        kn = [A.alloc(f"kn{i}", [512], BF16) for i in range(NKB)]
        kr = [A.alloc(f"kr{i}", [512], BF16) for i in range(NKB)]
        vv = [A.alloc(f"vv{i}", [4, 128], BF16) for i in range(NKB)]
        qn = [A.alloc(f"qn{i}", [512], BF16) for i in range(2)]
        qr = [A.alloc(f"qr{i}", [512], BF16) for i in range(2)]
        pT = [A.alloc(f"pT{i}", [512], BF16) for i in range(NPT)]
        rl = A.alloc("rl", [512], F32)
        attnT = A.alloc("attnT", [8, 512], F32)
        sqa = [A.alloc(f"sqa{i}", [512], BF16) for i in range(2)]
        ra = A.alloc("ra", [512], F32)
        mixT = [A.alloc(f"mixT{i}", [16, 512], BF16) for i in range(1)]
        xo = [A.alloc(f"xo{i}", [D], F32) for i in range(2)]
        x1 = [A.alloc(f"x1_{i}", [D], F32) for i in range(2)]
        junk = A.alloc("junk", [D], BF16)
        ss1 = [A.alloc(f"ss1_{i}", [1], F32) for i in range(2)]
        xb = [A.alloc(f"xb{i}", [D], BF16) for i in range(2)]
        xnTs = [A.alloc(f"xnTs{i}", [16, 512], BF16) for i in range(1)]
        kc_cnt = 0
        q_cnt = 0
        p_cnt = 0
        t_cnt = 0
        NPC = NPRE // 512
        for j in range(NGO):
            MX = mixT[0]
            dma("sp", MX.ap[:, 8:16, :], Rec_d[:, :, j * 512:(j + 1) * 512].rearrange("h p t -> p h t"), [], [MX.r])
            nchunks = NPC + j + 1
            for h in range(8):
                QN, QR = qn[q_cnt % 2], qr[q_cnt % 2]
                q_cnt += 1
                dma("sp", QN.ap, Qn_d[h, :, j * 512:(j + 1) * 512], [], [QN.r])
                dma("sp", QR.ap[0:64], Qr_d[h, :, j * 512:(j + 1) * 512], [], [QR.r])
                po, pl = PB[2], PB[3]
                items = [(kc, kb) for kc in range(nchunks) for kb in range(4)]
                cbuf = {}
                stt_ = {}

                def stS(i):
                    nonlocal kc_cnt, p_cnt
                    kc, kb = items[i]
                    if kb == 0:
                        KN, KR, VV = kn[kc_cnt % NKB], kr[kc_cnt % NKB], vv[kc_cnt % NKB]
                        kc_cnt += 1
                        k0 = kc * 512
                        dma("sp", KN.ap, Kn_d[h, :, k0:k0 + 512], [], [KN.r])
                        dma("sp", KR.ap[0:64], Kr_d[h, :, k0:k0 + 512], [], [KR.r])
                        dma("sp", VV.ap, V_d[k0:k0 + 512, h * 128:(h + 1) * 128].rearrange("(a p) d -> p a d", p=128),
                            [], [VV.r])
                        cbuf[kc] = (KN, KR, VV)
                    KN, KR, VV = cbuf[kc]
                    diag = kc == nchunks - 1
                    qlo = kb * 128 if diag else 0
                    psb = SB[p_cnt % NS]
                    PT = pT[p_cnt % NPT]
                    p_cnt += 1
                    mm(psb.ap[:, qlo:], KN.ap[:, kb * 128:(kb + 1) * 128], QN.ap[:, qlo:], True, False,
                       [KN.r, QN.r], [psb.r])
                    mm(psb.ap[:, qlo:], KR.ap[0:64, kb * 128:(kb + 1) * 128], QR.ap[0:64, qlo:], False, True,
                       [KR.r, QR.r], [psb.r])
                    if kc < NPC:
                        act(PT.ap[:, qlo:], psb.ap[:, qlo:], AF.Exp, [psb.r, flg.r], [PT.r], bias=flg.ap[:, 1:2])
                    else:
                        act(PT.ap[:, qlo:], psb.ap[:, qlo:], AF.Exp, [psb.r], [PT.r])
                    if diag:
                        tt("dve", PT.ap[:, qlo:qlo + 128], PT.ap[:, qlo:qlo + 128], tri.ap, ALU.mult,
                           [PT.r, tri.r], [PT.r])
                    stt_[i] = (PT, VV, qlo, kb)

                def stPV(i):
                    PT, VV, qlo, kb = stt_[i]
                    first = i == 0
                    last = i == len(items) - 1
                    mm(po.ap[:, qlo:], VV.ap[:, kb, :], PT.ap[:, qlo:], first, last, [VV.r, PT.r], [po.r])
                    mm(pl.ap[:, qlo:], ones.ap, PT.ap[:, qlo:], first, last, [ones.r, PT.r], [pl.r])

                for i in range(min(DEPTH, len(items))):
                    stS(i)
                for i in range(len(items)):
                    if i + DEPTH < len(items):
                        stS(i + DEPTH)
                    stPV(i)
                recip(rl.ap, pl.ap, [pl.r], [rl.r])
                tt("dve", attnT.ap[:, h, :], po.ap, rl.ap, ALU.mult, [po.r, rl.r], [attnT.r])
                SQ = sqa[h % 2]
                act(SQ.ap, attnT.ap[:, h, :], AF.Square, [attnT.r], [SQ.r])
                mm(PB[4].ap, ones.ap, SQ.ap, h == 0, h == 7, [ones.r, SQ.r], [PB[4].r])
            rstd_from_ps(ra, PB[4].ap, [PB[4].r], 1024)
            tt("dve", MX.ap[:, 0:8, :], attnT.ap, ra.ap[:, None, :].to_broadcast([128, 8, 512]), ALU.mult,
               [attnT.r, ra.r], [MX.r])
            XN = xnTs[0]
            for t4 in range(4):
                k = t_cnt
                t_cnt += 1
                XO, X1, S1, XB_ = xo[k % 2], x1[k % 2], ss1[k % 2], xb[k % 2]
                r0 = j * 512 + t4 * 128
                dma("sp", XO.ap, x_own[r0:r0 + 128, :], [], [XO.r])
                for n4 in range(4):
                    pb = PB[n4 % 2]
                    for c in range(16):
                        mm(pb.ap, MX.ap[:, c, t4 * 128:(t4 + 1) * 128], w_out_bf.ap[:, c, n4 * 512:(n4 + 1) * 512],
                           c == 0, c == 15, [MX.r, w_out_bf.r], [pb.r])
                    tt("dve", X1.ap[:, n4 * 512:(n4 + 1) * 512], pb.ap, XO.ap[:, n4 * 512:(n4 + 1) * 512], ALU.add,
                       [pb.r, XO.r], [X1.r])
                dma("pool", X1_d[r0:r0 + 128, :], X1.ap, [X1.r], [])
                P.add("pool", lambda e, S1=S1: e.memset(S1.ap, 0.0), [], [S1.r])
                act(junk.ap, X1.ap, AF.Square, [X1.r, S1.r], [junk.r, S1.r], accum=S1.ap)
                rstd_from_ps(S1, S1.ap, [S1.r], D)
                ts("dve", XB_.ap, X1.ap, S1.ap[:, 0:1], None, ALU.mult, None, [X1.r, S1.r], [XB_.r])
                pv = ps_bf(6, 2)
                for c in range(16):
                    tr(pv[:, c * 128:(c + 1) * 128], XB_.ap[:, c * 128:(c + 1) * 128], ident.ap,
                       [XB_.r, ident.r], [PB[6].r, PB[7].r])
                cp("act", XN.ap[:, :, t4 * 128:(t4 + 1) * 128], pv.rearrange("p (c t) -> p c t", c=16),
                   [PB[6].r, PB[7].r], [XN.r])
            dma("pool", XnT_d[:, :, j * 512:(j + 1) * 512].rearrange("c p t -> p c t"), XN.ap, [XN.r], [])

    def phase4a():
        w_q_bf = A.alloc("w_q_bf", [16, D], BF16)
        keysT = A.alloc("keysT", [16, 128], BF16)
        m1 = A.mark()
        stg = [A.alloc(f"stg{i}", [D], F32) for i in range(2)]
        for c in range(16):
            s = stg[c % 2]
            dma("sp", s.ap, w_q[c * 128:(c + 1) * 128, :], [], [s.r])
            wcast(c, w_q_bf.ap[:, c, :], s.ap, v128.ap[:, V_FFNG + c:V_FFNG + c + 1], [s.r, v128.r], [w_q_bf.r])
        kb16 = A.alloc("kb16", [8, 128], BF16)
        for jj, ksrc in enumerate((keys1, keys2)):
            s = stg[jj]
            sv = s.ap[:, 0:1024].rearrange("p (h d) -> p h d", h=8)
            dma("sp", sv, ksrc.rearrange("h n d -> n h d"), [], [s.r])
            cp("dve", kb16.ap, sv, [s.r], [kb16.r])
            pv = ps_bf(0, 1)
            for h in range(8):
                tr(pv[:, h * 128:(h + 1) * 128], kb16.ap[:, h, :], ident.ap, [kb16.r, ident.r], [PB[0].r])
            cp("act", keysT.ap.rearrange("p (h j) n -> p h j n", j=2)[:, :, jj, :],
               pv.rearrange("p (h n) -> p h n", h=8), [PB[0].r], [keysT.r])
        P.barrier()
        A.release(m1)
        xg = [A.alloc(f"xg{i}", [16, 512], BF16) for i in range(2)]
        qT = [A.alloc(f"qT{i}", [16, 512], BF16) for i in range(2)]
        sc = [A.alloc(f"sc{i}", [2048], F32) for i in range(2)]
        t_cnt = 0
        bk = 0
        for j in range(NGO):
            XG, QT = xg[j % 2], qT[j % 2]
            dma("sp", XG.ap, XnT_d[:, :, j * 512:(j + 1) * 512].rearrange("c p t -> p c t"), [], [XG.r])
            for m in range(16):
                pb = PB[bk % 4]
                bk += 1
                for c in range(16):
                    mm(pb.ap, w_q_bf.ap[:, c, m * 128:(m + 1) * 128], XG.ap[:, c, :], c == 0, c == 15,
                       [w_q_bf.r, XG.r], [pb.r])
                cp("act" if m % 2 == 0 else "dve", QT.ap[:, m, :], pb.ap, [pb.r], [QT.r])
            for t4 in range(4):
                SCB = sc[t_cnt % 2]
                t_cnt += 1
                for n4 in range(4):
                    pb = PB[4 + n4]
                    for mm_ in range(4):
                        m = n4 * 4 + mm_
                        mm(pb.ap[:, mm_ * 128:(mm_ + 1) * 128], QT.ap[:, m, t4 * 128:(t4 + 1) * 128], keysT.ap[:, m, :],
                           True, True, [QT.r, keysT.r], [pb.r])
                    cp("act" if n4 % 2 == 0 else "dve", SCB.ap[:, n4 * 512:(n4 + 1) * 512], pb.ap, [pb.r], [SCB.r])
                r0 = j * 512 + t4 * 128
                dma("pool", S_d[r0:r0 + 128, :], SCB.ap, [SCB.r], [])

    def phase4b():
        TH = 32
        sc = [A.alloc(f"sc{i}", [16, 128], F32) for i in range(2)]
        works = [A.alloc(f"wk{m}", [128], F32) for m in range(16)]
        works2 = [A.alloc(f"wkc{h}", [256], F32) for h in range(8)]
        topR = [Res(f"top{m}") for m in range(16)]
        idxR = [Res(f"idx{m}") for m in range(16)]
        ctopR = [Res(f"ctop{h}") for h in range(8)]
        top = A.alloc("top", [16, 16], F32)
        idx = A.alloc("idx", [16, 16], U32)
        idxf = A.alloc("idxf", [16, 16], BF16)
        cand = A.alloc("cand", [8, 256], F32)
        ctop = A.alloc("ctop", [8, 16], F32)
        negm = A.alloc("negm", [8], F32)
        ex = A.alloc("ex", [8, 256], F32)
        msk = A.alloc("msk", [8, 256], F32)
        zs = A.alloc("zs", [8], F32)
        Cb = A.alloc("Cb", [16, 128], BF16)
        idxT2 = [A.alloc(f"idxT{i}", [2, 128], F32) for i in range(2)]
        CT2 = [A.alloc(f"CT{i}", [16, 128], BF16) for i in range(2)]
        E1 = [A.alloc(f"E1_{i}", [TH, 128], BF16) for i in range(2)]
        E2 = [A.alloc(f"E2_{i}", [TH, 128], BF16) for i in range(2)]
        BD = [A.alloc(f"BD_{i}", [TH, 128], BF16) for i in range(2)]
        CE = [A.alloc(f"CE_{i}", [4, 128], BF16) for i in range(2)]
        Gs = [A.alloc(f"Gs{i}", [NI1, 128], BF16) for i in range(2)]
        ntile = NOWN // 128
        mrr = [((m % 2) * 8 + m // 2) for m in range(16)]
        st4 = {"e": 0, "g": 0}

        def front(ti):
            S = sc[ti % 2]
            idxT, CT = idxT2[ti % 2], CT2[ti % 2]
            dma("sp", S.ap, S_d[ti * 128:(ti + 1) * 128, :].rearrange("p (m n) -> p m n", m=16), [], [S.r])
            for m in range(16):
                rr = mrr[m]
                P.add("dve", lambda e, m=m, rr=rr, S=S: e.max(out=top.ap[:, rr, 0:8], in_=S.ap[:, m, :]), [S.r], [topR[rr]])
            for m in range(16):
                rr = mrr[m]
                P.add("dve", lambda e, m=m, rr=rr, S=S: e.match_replace(
                    out=works[m].ap, in_to_replace=top.ap[:, rr, 0:8], in_values=S.ap[:, m, :], imm_value=-1e30),
                    [S.r, topR[rr]], [works[m].r])
            for m in range(16):
                rr = mrr[m]
                P.add("dve", lambda e, m=m, rr=rr: e.max(out=top.ap[:, rr, 8:16], in_=works[m].ap), [works[m].r], [topR[rr]])
            for m in range(16):
                rr = mrr[m]
                P.add("dve", lambda e, m=m, rr=rr, S=S: e.max_index(out=idx.ap[:, rr, 0:8], in_max=top.ap[:, rr, 0:8],
                                                                    in_values=S.ap[:, m, :]), [topR[rr], S.r], [idxR[rr]])
            for m in range(16):
                rr = mrr[m]
                P.add("dve", lambda e, m=m, rr=rr, S=S: e.max_index(out=idx.ap[:, rr, 8:16], in_max=top.ap[:, rr, 8:16],
                                                                    in_values=S.ap[:, m, :]), [topR[rr], S.r], [idxR[rr]])
            cp("dve", idxf.ap, idx.ap, idxR, [idxf.r])
            cv = cand.ap.rearrange("p h (a b) -> p h a b", a=16)
            tt("dve", cv, top.ap[:, 0:8, :].unsqueeze(3).to_broadcast([128, 8, 16, 16]),
               top.ap[:, 8:16, :].unsqueeze(2).to_broadcast([128, 8, 16, 16]), ALU.add, topR, [cand.r])
            for h in range(8):
                P.add("dve", lambda e, h=h: e.max(out=ctop.ap[:, h, 0:8], in_=cand.ap[:, h, :]), [cand.r], [ctopR[h]])
            for h in range(8):
                P.add("dve", lambda e, h=h: e.match_replace(out=works2[h].ap, in_to_replace=ctop.ap[:, h, 0:8],
                                                            in_values=cand.ap[:, h, :], imm_value=-1e30),
                      [cand.r, ctopR[h]], [works2[h].r])
            for h in range(8):
                P.add("dve", lambda e, h=h: e.max(out=ctop.ap[:, h, 8:16], in_=works2[h].ap), [works2[h].r], [ctopR[h]])
            ts("dve", negm.ap, ctop.ap[:, :, 0], -1.0, None, ALU.mult, None, ctopR, [negm.r])
            for h in range(8):
                act(ex.ap[:, h, :], cand.ap[:, h, :], AF.Exp, [cand.r, negm.r], [ex.r], bias=negm.ap[:, h:h + 1])
            tt("dve", msk.ap, cand.ap, ctop.ap[:, :, 15:16].to_broadcast([128, 8, 256]), ALU.is_ge,
               [cand.r] + ctopR, [msk.r])
            tt("dve", ex.ap, ex.ap, msk.ap, ALU.mult, [ex.r, msk.r], [ex.r])
            P.add("dve", lambda e: e.reduce_sum(out=zs.ap, in_=ex.ap, axis=AX.X), [ex.r], [zs.r])
            recip(zs.ap, zs.ap, [zs.r], [zs.r])
            tt("dve", Cb.ap.rearrange("p a (h b) -> p h a b", h=8), ex.ap.rearrange("p h (a b) -> p h a b", a=16),
               zs.ap.unsqueeze(2).unsqueeze(3).to_broadcast([128, 8, 16, 16]), ALU.mult, [ex.r, zs.r], [Cb.r])
            pv = ps_bf(0, 1)
            for jj in range(2):
                tr(pv[:, jj * 128:(jj + 1) * 128], idxf.ap[:, jj * 8:(jj + 1) * 8, :].rearrange("p h k -> p (h k)"),
                   ident.ap, [idxf.r, ident.r], [PB[0].r])
            cp("act", idxT.ap, pv[:, 0:256].rearrange("p (j t) -> p j t", j=2), [PB[0].r], [idxT.r])
            pv2 = ps_bf(1, 2)
            for k1 in range(16):
                tr(pv2[:, k1 * 128:(k1 + 1) * 128], Cb.ap[:, k1, :], ident.ap, [Cb.r, ident.r], [PB[1].r, PB[2].r])
            cp("act", CT.ap, pv2.rearrange("p (a t) -> p a t", a=16), [PB[1].r, PB[2].r], [CT.r])

        def back(ti):
            idxT, CT = idxT2[ti % 2], CT2[ti % 2]
            GS = Gs[ti % 2]
            NQ = TH // 4
            chunks = {}

            def egen(hh):
                tlo = hh * TH
                e1, e2, bd = E1[st4["e"] % 2], E2[st4["e"] % 2], BD[st4["e"] % 2]
                st4["e"] += 1
                tt("dve", e1.ap, iota_f.ap[:, None, :].to_broadcast([128, TH, 128]),
                   idxT.ap[:, 0, tlo:tlo + TH].unsqueeze(2).to_broadcast([128, TH, 128]), ALU.is_equal,
                   [iota_f.r, idxT.r], [e1.r])
                tt("dve", e2.ap, iota_f.ap[:, None, :].to_broadcast([128, TH, 128]),
                   idxT.ap[:, 1, tlo:tlo + TH].unsqueeze(2).to_broadcast([128, TH, 128]), ALU.is_equal,
                   [iota_f.r, idxT.r], [e2.r])
                tt("dve", bd.ap.rearrange("p t (h k) -> p t h k", h=8),
                   CT.ap[:, :, tlo:tlo + TH].rearrange("p k t -> p t k").unsqueeze(2).to_broadcast([128, TH, 8, 16]),
                   bmask.ap[:, None, :].unsqueeze(3).to_broadcast([128, TH, 8, 16]), ALU.mult,
                   [CT.r, bmask.r], [bd.r])
                chunks[hh] = (e1, e2, bd)

            items = [(hh, q4) for hh in range(128 // TH) for q4 in range(NQ)]
            slot = {}

            def stA(k):
                hh, q4 = items[k]
                if q4 == 0:
                    egen(hh)
                e1, e2, bd = chunks[hh]
                g = st4["g"]
                st4["g"] += 1
                pa, pg, ce = PB[3 + (g % 2)], PB[5 + (g % 2)], CE[g % 2]
                slot[k] = (pa, pg, ce)
                for i4 in range(4):
                    tl = q4 * 4 + i4
                    mm(pa.ap[:, i4 * 128:(i4 + 1) * 128], bd.ap[:, tl, :], e2.ap[:, tl, :], True, True,
                       [bd.r, e2.r], [pa.r])
                cp("act", ce.ap, pa.ap.rearrange("p (a n) -> p a n", a=4), [pa.r], [ce.r])

            def stB(k):
                hh, q4 = items[k]
                e1, e2, bd = chunks[hh]
                pa, pg, ce = slot[k]
                for i4 in range(4):
                    tl = q4 * 4 + i4
                    mm(pg.ap[:, i4 * 128:(i4 + 1) * 128], ce.ap[:, i4, :], e1.ap[:, tl, :], True, True,
                       [ce.r, e1.r], [pg.r])
                t0 = hh * TH + q4 * 4
                cp("act", GS.ap[:, :, t0:t0 + 4].rearrange("p i t -> p t i"),
                   pg.ap.rearrange("p (a n) -> p a n", a=4), [pg.r], [GS.r])

            stA(0)
            for k in range(len(items)):
                if k + 1 < len(items):
                    stA(k + 1)
                stB(k)
            gflat = GS.ap.rearrange("p i t -> p (i t)")
            NSPL = 4
            wd = NI1 * 128 // NSPL
            for sp_ in range(NSPL):
                dma("pool", G_d[ti][:, sp_ * wd:(sp_ + 1) * wd], gflat[:, sp_ * wd:(sp_ + 1) * wd], [GS.r], [])

        front(0)
        for ti in range(ntile):
            if ti + 1 < ntile:
                front(ti + 1)
            back(ti)

    def phase5():
        EB = 4
        xg = [A.alloc(f"xg{i}", [16, 512], BF16) for i in range(2)]
        acc = [A.alloc(f"acc{i}", [4, D], F32) for i in range(2)]
        uT = [A.alloc(f"uT{i}", [16, EB * 128], BF16) for i in range(2)]
        vb = [A.alloc(f"vbk{i}", [EB, D], BF16) for i in range(3)]
        gb = [A.alloc(f"gb{i}", [EB, 512], BF16) for i in range(2)]
        gl = [A.alloc(f"gl{i}", [512], BF16) for i in range(2)]
        W = [A.alloc(f"W{i}", [EB, 512], BF16) for i in range(2)]
        outs = []
        NEB = NI1 // EB
        blocks = [(j, eb) for j in range(NGO) for eb in range(NEB)]
        st5 = {"g": 0, "o": 0}

        def stageA(k):
            j, eb = blocks[k]
            XG, AC = xg[j % 2], acc[j % 2]
            if eb == 0:
                dma("sp", XG.ap, XnT_d[:, :, j * 512:(j + 1) * 512].rearrange("c p t -> p c t"), [], [XG.r])
                dma("sp", AC.ap, X1_d[j * 512:(j + 1) * 512, :].rearrange("(a p) d -> p a d", p=128), [], [AC.r])
            UT, VB, GB, WB = uT[k % 2], vb[k % 3], gb[k % 2], W[k % 2]
            e0 = eb * EB * 128
            dma("sp", UT.ap.rearrange("p c e -> p (c e)"), uT_d[eb], [], [UT.r])
            dma("sp", VB.ap, vb_d[e0:e0 + EB * 128, :].rearrange("(a p) d -> p a d", p=128), [], [VB.r])
            for t4 in range(4):
                dma("sp", GB.ap[:, :, t4 * 128:(t4 + 1) * 128],
                    G_d[j * 4 + t4].rearrange("p (i t) -> p i t", t=128)[:, eb * EB:(eb + 1) * EB, :], [], [GB.r])
            for et in range(EB):
                pa = PB[st5["g"] % 2]
                GL = gl[st5["g"] % 2]
                st5["g"] += 1
                for c in range(16):
                    mm(pa.ap, UT.ap[:, c, et * 128:(et + 1) * 128], XG.ap[:, c, :], c == 0, c == 15,
                       [UT.r, XG.r], [pa.r])
                act(GL.ap, pa.ap, AF.Gelu, [pa.r], [GL.r])
                tt("dve", WB.ap[:, et, :], GL.ap, GB.ap[:, et, :], ALU.mult, [GL.r, GB.r], [WB.r])

        def stageB(k):
            j, eb = blocks[k]
            AC = acc[j % 2]
            VB, WB = vb[k % 3], W[k % 2]
            for t4 in range(4):
                for hf in range(2):
                    oc = st5["o"]
                    st5["o"] += 1
                    po = [PB[2 + (oc % 3) * 2], PB[3 + (oc % 3) * 2]]
                    for et in range(EB):
                        for n2 in range(2):
                            col = hf * 1024 + n2 * 512
                            mm(po[n2].ap, WB.ap[:, et, t4 * 128:(t4 + 1) * 128], VB.ap[:, et, col:col + 512],
                               et == 0, et == EB - 1, [WB.r, VB.r], [po[n2].r])
                    for n2 in range(2):
                        col = hf * 1024 + n2 * 512
                        tt("dve", AC.ap[:, t4, col:col + 512], po[n2].ap, AC.ap[:, t4, col:col + 512], ALU.add,
                           [po[n2].r, AC.r], [AC.r])
            if eb == NEB - 1:
                outs.append(dma("pool", out_d[j * 512:(j + 1) * 512, :].rearrange("(a p) d -> p a d", p=128), AC.ap,
                                [AC.r], []))

        stageA(0)
        for k in range(len(blocks)):
            if k + 1 < len(blocks):
                stageA(k + 1)
            stageB(k)
        return outs

    stop_after = [k for k in debug if isinstance(k, str) and k.startswith("stop:")]
    stop = stop_after[0][5:] if stop_after else None
    finals = []

    def run_phase(fn):
        m = A.mark()
        r = fn()
        P.barrier()
        A.release(m)
        return r

    seq = [("tables", phase_tables), ("p1", phase1), ("p23", phase23), ("p4a", phase4a), ("p4b", phase4b),
           ("p5", phase5)]
    skip = set(k[5:] for k in debug if isinstance(k, str) and k.startswith("skip:"))
    last = None
    for name, fn in seq:
        if name in skip:
            continue
        last = run_phase(fn)
        if stop == name:
            break
    if last:
        finals = [("pool", o) for o in last]
    else:
        dummy = A.alloc("dummy", [16], F32)
        P.add("dve", lambda e: e.memset(dummy.ap, 0.0), [], [dummy.r])
        o = dma("pool", out_d[0:128, 0:16], dummy.ap, [dummy.r], [])
        finals = [("pool", o)]
    P.emit(st, final_wait_ops=finals)
    st.close()
    return nc


def _cols(v, n):
    return np.ascontiguousarray(np.asarray(v, np.float32).reshape(n, 128).T)


def make_core_inputs(inp, b, tok0, NOWN, NPRE, has_prefix, NE=16384):
    f = np.float32
    x = inp["x"]
    L = 0
    v128 = np.zeros((128, NV128), f)
    v128[:, V_MIXG:V_MIXG + 16] = _cols(inp["mix_norm_g"][L], 16)
    v128[:, V_FFNG:V_FFNG + 16] = _cols(inp["ffn_norm_g"][L], 16)
    v128[:, V_QNG:V_QNG + 4] = _cols(inp["q_norm_g"][L], 4)
    v128[:, V_KVNG:V_KVNG + 2] = _cols(inp["kv_norm_g"][L], 2)
    v128[:, V_AOG:V_AOG + 8] = _cols(inp["attn_out_norm_g"][L], 8)
    v128[:, V_ROG:V_ROG + 8] = _cols(inp["rec_out_norm_g"][L], 8)
    v128[:, V_QHG] = inp["q_head_norm_g"][L][:128]
    v128[:, V_KHG] = inp["k_head_norm_g"][L][:128]
    for j in range(4):
        v128[:, V_CW + j * 8:V_CW + j * 8 + 8] = _cols(inp["conv_w"][L][j], 8)
    v128[:, V_CB:V_CB + 8] = _cols(inp["conv_b"][L], 8)
    v128[:, V_BRG:V_BRG + 8] = _cols(inp["b_rgate"][L], 8)
    v128[:, V_BIG:V_BIG + 8] = _cols(inp["b_igate"][L], 8)
    v128[:, V_LAM:V_LAM + 8] = _cols(inp["lru_lambda"][L], 8)
    v64 = np.zeros((64, NV64), f)
    qg = np.asarray(inp["q_head_norm_g"][L], f)[128:192]
    kg = np.asarray(inp["k_head_norm_g"][L], f)[128:192]
    v64[:, W_QHG] = qg
    v64[:, W_QHGS] = np.concatenate([qg[32:], qg[:32]])
    v64[:, W_KHG] = kg
    v64[:, W_KHGS] = np.concatenate([kg[32:], kg[:32]])
    invf = (np.float32(10000.0) ** (-np.arange(32, dtype=np.float32) / np.float32(32))).astype(f)
    v64[:, W_INVF] = np.concatenate([invf, invf])
    v64[:, W_SIGN] = np.concatenate([-np.ones(32, f), np.ones(32, f)])
    flags = np.zeros((128, 2), f)
    flags[:, 0] = 1.0 if has_prefix else 0.0
    flags[:, 1] = 0.0 if has_prefix else NEG
    pre0 = tok0 - NPRE if has_prefix else 0
    pos = np.concatenate([inp["positions"][b, pre0:pre0 + NPRE], inp["positions"][b, tok0:tok0 + NOWN]])
    return {
        "x_own": np.ascontiguousarray(x[b, tok0:tok0 + NOWN]),
        "x_pre": np.ascontiguousarray(x[b, pre0:pre0 + NPRE]),
        "pos_all": np.ascontiguousarray(pos.astype(np.int32).reshape(1, -1)),
        "flags": flags,
        "vec128": v128,
        "vec64": v64,
        "ffng_row": np.ascontiguousarray(np.asarray(inp["ffn_norm_g"][L], f).reshape(1, D)),
        "w_in": np.ascontiguousarray(inp["w_in"][L]),
        "w_uq": np.ascontiguousarray(inp["w_uq"][L]),
        "w_ukv": np.ascontiguousarray(inp["w_ukv"][L]),
        "w_out": np.ascontiguousarray(inp["w_out"][L]),
        "w_q": np.ascontiguousarray(inp["peer_w_q"][L]),
        "w_rg": np.ascontiguousarray(inp["w_rgate"][L]),
        "w_ig": np.ascontiguousarray(inp["w_igate"][L]),
        "keys1": np.ascontiguousarray(inp["peer_keys_1"][L]),
        "keys2": np.ascontiguousarray(inp["peer_keys_2"][L]),
        "u_tab": np.ascontiguousarray(inp["peer_u"][L]),
        "v_tab": np.ascontiguousarray(inp["peer_v"][L]),
    }


def kernel(**inputs):
    inp = {k: np.asarray(v) for k, v in inputs.items()}
    B, S, _ = inp["x"].shape
    NOWN = S // 2
    NPRE = S // 2
    nc = build(NOWN, NPRE)
    in_maps = []
    for c in range(2 * B):
        b, half = c // 2, c % 2
        in_maps.append(make_core_inputs(inp, b, half * NOWN, NOWN, NPRE, half == 1))
    res = run_bass_kernel_spmd(nc, in_maps, core_ids=list(range(2 * B)))
    out = np.empty((B, S, D), np.float32)
    for c in range(2 * B):
        b, half = c // 2, c % 2
        out[b, half * NOWN:(half + 1) * NOWN] = res.results[c]["out"]
    return out
```
